# Optimizing a Trainium2 kernel written in Bass

```python
import jax, jax.numpy as jnp
from jax import lax
import numpy as np

D_MODEL = 1024
BATCH = 16
SEQ = 2048
DEPTH = 2

GRID_W = 64
CTX_LEN = 256
N_MIXERS = 2
N_HEADS = 16
N_KV_HEADS = 4
HEAD_DIM = 64
GROUP = N_HEADS // N_KV_HEADS
Q_WIDTH = N_HEADS * HEAD_DIM
KV_WIDTH = N_KV_HEADS * HEAD_DIM
WINDOW = 128
BLOCK = 128
SPAN = BLOCK + 2 * WINDOW
ROPE_BASE = 10000.0
AXIS_DIM = HEAD_DIM // 2
FOURIER_GROUPS = 4
FOURIER_GROUP_CH = D_MODEL // FOURIER_GROUPS
D_FF = 2816
N_MOD = 9
LN_EPS = 1e-5
ALPHA = (2.0 * DEPTH) ** 0.25
BETA = (8.0 * DEPTH) ** -0.25
NEG_INF = -1e30
N_ATTN_LAYERS = (DEPTH + N_MIXERS - 1) // N_MIXERS
N_FOURIER_LAYERS = (DEPTH + N_MIXERS - 2) // N_MIXERS

kernel_name = "hybrid_window_sink_gqa_fnet_macaron_deepnorm"


def layer_norm(x, g, b):
    xf = x.astype(jnp.float32)
    mu = xf.mean(-1, keepdims=True)
    var = jnp.square(xf - mu).mean(-1, keepdims=True)
    return ((xf - mu) * lax.rsqrt(var + LN_EPS) * g + b).astype(x.dtype)


def modulate(x, mod, j):
    shift, scale = mod[:, 3 * j], mod[:, 3 * j + 1]
    return x * (1 + scale[:, None, :]) + shift[:, None, :]


def post_norm_update(x, y, mod, j, g, b):
    gate = mod[:, 3 * j + 2][:, None, :]
    return layer_norm(ALPHA * x + gate * y, g, b)


def swiglu(h, wi, wo):
    a, g = jnp.split(h @ wi, 2, axis=-1)
    return (a * jax.nn.silu(g)) @ wo


def half_ffn_sublayer(x, mod, j, wi, wo, g, b):
    y = 0.5 * swiglu(modulate(x, mod, j), wi, wo)
    return post_norm_update(x, y, mod, j, g, b)


def axial_rope_tables(n_tokens):
    rows = n_tokens // GRID_W
    row = jnp.repeat(jnp.arange(rows), GRID_W).astype(jnp.float32)
    col = jnp.tile(jnp.arange(GRID_W), rows).astype(jnp.float32)
    inv = ROPE_BASE ** (-jnp.arange(0, AXIS_DIM, 2, dtype=jnp.float32) / AXIS_DIM)
    ang = jnp.concatenate([row[:, None] * inv, col[:, None] * inv], axis=-1)
    return jnp.cos(ang), jnp.sin(ang)


def apply_axial_rope(x, cos, sin):
    B_, L_, H_, Dh = x.shape
    half = AXIS_DIM // 2
    xr = x.astype(jnp.float32).reshape(B_, L_, H_, 2, 2, half)
    x1, x2 = xr[..., 0, :], xr[..., 1, :]
    c = cos.reshape(L_, 1, 2, half)
    s = sin.reshape(L_, 1, 2, half)
    out = jnp.stack([x1 * c - x2 * s, x1 * s + x2 * c], axis=-2)
    return out.reshape(B_, L_, H_, Dh).astype(x.dtype)


def window_sink_attention(h, hc, w_qkv, b_qkv, w_o, sink, cos, sin, ctx_queries):
    B_, S_, _ = h.shape
    C_ = hc.shape[1]
    n_blocks = S_ // BLOCK
    scale = HEAD_DIM ** -0.5
    q, k, v = jnp.split(h @ w_qkv + b_qkv, [Q_WIDTH, Q_WIDTH + KV_WIDTH], axis=-1)
    q = apply_axial_rope(q.reshape(B_, S_, N_HEADS, HEAD_DIM) * scale, cos, sin)
    q = q.reshape(B_, S_, N_KV_HEADS, GROUP, HEAD_DIM)
    k = apply_axial_rope(k.reshape(B_, S_, N_KV_HEADS, HEAD_DIM), cos, sin)
    v = v.reshape(B_, S_, N_KV_HEADS, HEAD_DIM)
    kvc = hc @ w_qkv[:, Q_WIDTH:] + b_qkv[Q_WIDTH:]
    kc, vc = jnp.split(kvc, [KV_WIDTH], axis=-1)
    kc = kc.reshape(B_, C_, N_KV_HEADS, HEAD_DIM)
    vc = vc.reshape(B_, C_, N_KV_HEADS, HEAD_DIM)

    pad = ((0, 0), (WINDOW, WINDOW), (0, 0), (0, 0))
    k_pad = jnp.pad(k, pad)
    v_pad = jnp.pad(v, pad)
    sink_logit = sink.astype(jnp.float32).reshape(1, N_KV_HEADS, GROUP, 1, 1)
    rel = jnp.arange(BLOCK)[:, None] - jnp.arange(SPAN)[None, :] + WINDOW
    band = jnp.abs(rel) <= WINDOW

    def block(bi):
        start = bi * BLOCK
        q_b = lax.dynamic_slice_in_dim(q, start, BLOCK, axis=1)
        k_b = lax.dynamic_slice_in_dim(k_pad, start, SPAN, axis=1)
        v_b = lax.dynamic_slice_in_dim(v_pad, start, SPAN, axis=1)
        key_pos = start - WINDOW + jnp.arange(SPAN)
        valid = band & ((key_pos >= 0) & (key_pos < S_))[None, :]
        s_win = jnp.einsum('bqhgd,bkhd->bhgqk', q_b, k_b, preferred_element_type=jnp.float32)
        s_win = jnp.where(valid, s_win, NEG_INF)
        s_ctx = jnp.einsum('bqhgd,bkhd->bhgqk', q_b, kc, preferred_element_type=jnp.float32)
        sinks = jnp.broadcast_to(sink_logit, s_ctx.shape[:-1] + (1,))
        p = jax.nn.softmax(jnp.concatenate([s_win, s_ctx, sinks], axis=-1), axis=-1).astype(v.dtype)
        o = (jnp.einsum('bhgqk,bkhd->bqhgd', p[..., :SPAN], v_b)
             + jnp.einsum('bhgqk,bkhd->bqhgd', p[..., SPAN:SPAN + C_], vc))
        return o

    o = lax.map(block, jnp.arange(n_blocks))
    o = jnp.moveaxis(o, 0, 1).reshape(B_, S_, Q_WIDTH)
    y = o @ w_o
    if not ctx_queries:
        return y, None
    qc = (hc @ w_qkv[:, :Q_WIDTH] + b_qkv[:Q_WIDTH]) * scale
    qc = qc.reshape(B_, C_, N_KV_HEADS, GROUP, HEAD_DIM)
    sc = jnp.einsum('bqhgd,bkhd->bhgqk', qc, kc, preferred_element_type=jnp.float32)
    sinks_c = jnp.broadcast_to(sink_logit, sc.shape[:-1] + (1,))
    pc = jax.nn.softmax(jnp.concatenate([sc, sinks_c], axis=-1), axis=-1).astype(vc.dtype)
    oc = jnp.einsum('bhgqk,bkhd->bqhgd', pc[..., :C_], vc).reshape(B_, C_, Q_WIDTH)
    return y, oc @ w_o


def fourier_mix(h, w_o):
    B_, L_, D_ = h.shape
    hg = h.astype(jnp.float32).reshape(B_, L_, FOURIER_GROUPS, FOURIER_GROUP_CH)
    f = jnp.fft.fft2(hg, axes=(1, 3), norm="ortho").real
    return f.reshape(B_, L_, D_).astype(h.dtype) @ w_o


def setup_inputs(seed: int = 0) -> dict:
    key = jax.random.key(seed)
    ks = jax.random.split(key, 16)

    def nrm(k, shape, s):
        return jax.random.normal(k, shape, jnp.float32) * s

    qkv_w = Q_WIDTH + 2 * KV_WIDTH
    return {
        "x": nrm(ks[0], (BATCH, SEQ, D_MODEL), 1.0),
        "c": nrm(ks[1], (BATCH, D_MODEL), 1.0),
        "ctx": nrm(ks[2], (BATCH, CTX_LEN, D_MODEL), 1.0),
        "c_ctx": nrm(ks[3], (D_MODEL,), 1.0),
        "mod_w": nrm(ks[4], (DEPTH, D_MODEL, N_MOD * D_MODEL), D_MODEL ** -0.5),
        "mod_b": nrm(ks[5], (DEPTH, N_MOD * D_MODEL), 0.02),
        "ln_g": 1.0 + nrm(ks[6], (DEPTH, 3, D_MODEL), 0.02),
        "ln_b": nrm(ks[7], (DEPTH, 3, D_MODEL), 0.02),
        "ffn_wi": nrm(ks[8], (DEPTH, 2, D_MODEL, 2 * D_FF), D_MODEL ** -0.5),
        "ffn_wo": nrm(ks[9], (DEPTH, 2, D_FF, D_MODEL), BETA * D_FF ** -0.5),
        "attn_wqkv": nrm(ks[10], (N_ATTN_LAYERS, D_MODEL, qkv_w), D_MODEL ** -0.5),
        "attn_bqkv": nrm(ks[11], (N_ATTN_LAYERS, qkv_w), 0.02),
        "attn_wo": nrm(ks[12], (N_ATTN_LAYERS, Q_WIDTH, D_MODEL), BETA * Q_WIDTH ** -0.5),
        "attn_sink": nrm(ks[13], (N_ATTN_LAYERS, N_HEADS), 1.0),
        "fourier_wo": nrm(ks[14], (N_FOURIER_LAYERS, D_MODEL, D_MODEL), BETA * D_MODEL ** -0.5),
    }


def reference(x, c, ctx, c_ctx, mod_w, mod_b, ln_g, ln_b, ffn_wi, ffn_wo,
              attn_wqkv, attn_bqkv, attn_wo, attn_sink, fourier_wo):
    cos, sin = axial_rope_tables(x.shape[1])
    silu_c = jax.nn.silu(c)
    silu_cc = jax.nn.silu(c_ctx)
    ctx_s = ctx
    for i in range(DEPTH):
        kind = i % N_MIXERS
        ctx_out_live = any(j % N_MIXERS == 0 for j in range(i + 1, DEPTH))
        ctx_in_live = (kind == 0) or ctx_out_live
        g, b = ln_g[i], ln_b[i]
        mod_lat = (silu_c @ mod_w[i] + mod_b[i]).reshape(-1, N_MOD, D_MODEL)
        x = half_ffn_sublayer(x, mod_lat, 0, ffn_wi[i, 0], ffn_wo[i, 0], g[0], b[0])
        if ctx_in_live:
            mod_ctx = (silu_cc @ mod_w[i] + mod_b[i]).reshape(1, N_MOD, D_MODEL)
            ctx_s = half_ffn_sublayer(ctx_s, mod_ctx, 0, ffn_wi[i, 0], ffn_wo[i, 0], g[0], b[0])
        h = modulate(x, mod_lat, 1)
        if kind == 0:
            a = i // N_MIXERS
            y, yc = window_sink_attention(h, modulate(ctx_s, mod_ctx, 1), attn_wqkv[a], attn_bqkv[a],
                                          attn_wo[a], attn_sink[a], cos, sin, ctx_out_live)
        else:
            f = i // N_MIXERS
            y = fourier_mix(h, fourier_wo[f])
            yc = fourier_mix(modulate(ctx_s, mod_ctx, 1), fourier_wo[f]) if ctx_out_live else None
        x = post_norm_update(x, y, mod_lat, 1, g[1], b[1])
        x = half_ffn_sublayer(x, mod_lat, 2, ffn_wi[i, 1], ffn_wo[i, 1], g[2], b[2])
        if ctx_out_live:
            ctx_s = post_norm_update(ctx_s, yc, mod_ctx, 1, g[1], b[1])
            ctx_s = half_ffn_sublayer(ctx_s, mod_ctx, 2, ffn_wi[i, 1], ffn_wo[i, 1], g[2], b[2])
    return x
```

```python
import os
import numpy as np
import ml_dtypes
from contextlib import ExitStack
import concourse.bass as bass
import concourse.mybir as mybir
from concourse.bass_utils import run_bass_kernel_spmd

F32 = mybir.dt.float32
F32R = mybir.dt.float32r
BF16 = mybir.dt.bfloat16
ALU = mybir.AluOpType
AF = mybir.ActivationFunctionType

ENGS = ("pe", "act", "dve", "pool", "sp")
NCORES = 8
D = 1024
S = 2048
TOK = 4096
CTXN = 512
DFF = 2816
NJ = 22
W = 256
WTF = 512


def _dbg(name, default=None):
    if os.environ.get("KERNEL_DEBUG_HOOKS") != "1":
        return default
    return os.environ.get(name, default)
ALPHA = 4.0 ** 0.25
LN_EPS = 1e-5


class Buf:
    __slots__ = ("name", "last_w", "readers")

    def __init__(self, name):
        self.name = name
        self.last_w = None
        self.readers = []


class Op:
    __slots__ = ("eng", "fn", "deps", "is_dma", "key", "needs_signal", "tok", "kind", "name")

    def __init__(self, eng, fn, is_dma=False, key=None, kind="op", name=""):
        self.eng = eng
        self.fn = fn
        self.deps = []
        self.is_dma = is_dma
        self.key = key
        self.needs_signal = False
        self.tok = None
        self.kind = kind
        self.name = name


class Prog:
    def __init__(self, nc, stack):
        self.nc = nc
        self.stack = stack
        self.ops = {e: [] for e in ENGS}
        self.bufs = []
        self.sem = {}
        self.cnt = {}
        self.waited = {e: {} for e in ENGS}
        self.epoch_dma = {}
        for e in ("pe", "act", "dve", "pool"):
            self._mksem("prog_" + e)
        self._mksem("bar")

    def _mksem(self, name):
        s = self.stack.enter_context(self.nc.semaphore(name))
        self.sem[name] = s
        self.cnt[name] = 0
        return s

    def buf(self, name=""):
        b = Buf(name)
        self.bufs.append(b)
        return b

    def bufs_n(self, n, name=""):
        return [self.buf(f"{name}{i}") for i in range(n)]

    def _record(self, o, reads, writes):
        deps = []
        for b in reads:
            if b.last_w is not None:
                deps.append(b.last_w)
        for b in writes:
            if b.last_w is not None:
                deps.append(b.last_w)
            deps.extend(b.readers)
        seen = set()
        for d in deps:
            if d is o or id(d) in seen:
                continue
            seen.add(id(d))
            if (not d.is_dma) and (not o.is_dma) and d.eng == "pe" and o.eng == "pe":
                continue
            o.deps.append(d)
            d.needs_signal = True
        for b in reads:
            b.readers.append(o)
        for b in writes:
            b.last_w = o
            b.readers = []
        self.ops[o.eng].append(o)
        return o

    def op(self, eng, fn, reads=(), writes=(), name=""):
        return self._record(Op(eng, fn, name=name), reads, writes)

    def dma(self, queue, out, in_, reads=(), writes=(), key=None):
        k = "dma_" + key
        if k not in self.sem:
            self._mksem(k)
        o = Op(queue, lambda e: e.dma_start(out=out, in_=in_), is_dma=True, key=k)
        self._record(o, reads, writes)
        o.needs_signal = True
        self.epoch_dma[k] = o
        return o

    def barrier(self):
        b = Op("sp", None, kind="barrier")
        for e in ("pe", "act", "dve", "pool"):
            for o in reversed(self.ops[e]):
                if o.kind == "op" and not o.is_dma:
                    o.needs_signal = True
                    b.deps.append(o)
                    break
        for o in self.epoch_dma.values():
            b.deps.append(o)
        self.epoch_dma = {}
        self.ops["sp"].append(b)
        for e in ("pe", "act", "dve", "pool"):
            w = Op(e, None, kind="barwait")
            w.deps.append(b)
            self.ops[e].append(w)
        for bf in self.bufs:
            bf.last_w = None
            bf.readers = []
        self.bufs = []

    def _assign(self):
        for e in ENGS:
            for o in self.ops[e]:
                if o.tok is not None:
                    continue
                if o.kind == "barrier":
                    self.cnt["bar"] += 1
                    o.tok = ("bar", self.cnt["bar"])
                elif o.kind == "barwait":
                    o.tok = ("none", 0)
                elif o.is_dma:
                    self.cnt[o.key] += 16
                    o.tok = (o.key, self.cnt[o.key])
                elif o.needs_signal:
                    k = "prog_" + e
                    self.cnt[k] += 1
                    o.tok = (k, self.cnt[k])
                else:
                    o.tok = ("none", 0)

    def _emit_eng(self, e, eng):
        waited = self.waited[e]
        for o in self.ops[e]:
            need = {}
            for d in o.deps:
                k, v = d.tok
                assert k != "none", (o.name, d.name)
                if v > need.get(k, 0):
                    need[k] = v
            for k, v in need.items():
                if waited.get(k, 0) >= v:
                    continue
                eng.wait_ge(self.sem[k], v)
                waited[k] = v
            if o.kind == "barrier":
                eng.sem_inc(self.sem["bar"], 1)
                continue
            if o.kind == "barwait":
                continue
            ins = o.fn(eng)
            k, v = o.tok
            if k != "none":
                ins.then_inc(self.sem[k], 16 if o.is_dma else 1)

    def flush(self, name=None):
        self._assign()
        self.nflush = getattr(self, "nflush", 0) + 1
        with self.nc.named_scope(name or f"ph{self.nflush}"), self.nc.Block() as block:
            @block.tensor
            def _(eng):
                self._emit_eng("pe", eng)

            @block.scalar
            def _(eng):
                self._emit_eng("act", eng)

            @block.vector
            def _(eng):
                self._emit_eng("dve", eng)

            @block.gpsimd
            def _(eng):
                self._emit_eng("pool", eng)

            @block.sync
            def _(eng):
                self._emit_eng("sp", eng)
        self.ops = {e: [] for e in ENGS}


def MM(P, out, lhsT, rhs, start, stop, reads, writes):
    return P.op("pe", lambda e: e.matmul(out, lhsT, rhs, start=start, stop=stop), reads, writes)


def ACT(P, out, in_, func, reads, writes, bias=0.0, scale=1.0):
    return P.op("act", lambda e: e.activation(out, in_, func, bias=bias, scale=scale), reads, writes)


def TT(P, eng, out, in0, in1, op, reads, writes):
    return P.op(eng, lambda e: e.tensor_tensor(out, in0, in1, op), reads, writes)


def TS(P, eng, out, in0, s1, s2, op0, op1, reads, writes):
    return P.op(eng, lambda e: e.tensor_scalar(out, in0, s1, s2, op0, op1), reads, writes)


def STT(P, eng, out, in0, scalar, in1, op0, op1, reads, writes):
    return P.op(eng, lambda e: e.scalar_tensor_tensor(out, in0, scalar, in1, op0, op1), reads, writes)


def bc_mid(ap2d, n):
    return ap2d.unsqueeze(1).broadcast_to([ap2d.shape[0], n, ap2d.shape[1]])


def bc_last(ap2d, n):
    return ap2d.unsqueeze(2).broadcast_to([ap2d.shape[0], ap2d.shape[1], n])


class K:
    pass


def build(upto=99, debug=False):
    nc = bass.Bass("TRN2", target_bir_lowering=False)
    k = K()
    k.nc = nc

    def din(name, shape, dt=F32):
        return nc.dram_tensor(name, list(shape), dt, kind="ExternalInput").ap()

    skind = "ExternalOutput" if debug else "Internal"

    def dscr(name, shape, dt):
        return nc.dram_tensor(name, list(shape), dt, kind=skind).ap()

    k.xT = din("xT", [8, 128, 4096])
    k.ctxT = din("ctxT", [1, 128, 4096])
    k.cT = din("cT", [128, 24])
    k.mod_w = din("mod_w", [2, D, 9 * D])
    k.mod_bT = din("mod_bT", [128, 144])
    k.lngT = din("lngT", [128, 48])
    k.lnbT = din("lnbT", [128, 48])
    k.ffn_wi = din("ffn_wi", [4, D, 2 * DFF])
    k.ffn_wo = din("ffn_wo", [4, DFF, D])
    k.wqkv = din("wqkv", [D, 3328])
    k.bqk = din("bqk", [128, 24])
    k.bv_bc = din("bv_bc", [128, 256])
    k.attn_wo = din("attn_wo", [D, D])
    k.sink_bc = din("sink_bc", [128, 16])
    k.four_wo = din("four_wo", [D, D])
    k.cosT = din("cosT", [128, S])
    k.sinT = din("sinT", [128, S])
    k.masks = din("masks", [128, 256], BF16)
    k.CL = din("CL", [S, S], BF16)
    k.SLn = din("SLn", [S, S], BF16)
    k.CSC = din("CSC", [256, 512], BF16)
    k.FM = din("FM", [128, 640], BF16)
    k.SW = din("SW", [128, 128], BF16)
    k.yT = nc.dram_tensor("yT", [8, 128, 4096], F32, kind="ExternalOutput").ap()
    k.xa_s = dscr("xa_s", [8, 128, 4096], F32)
    k.hT_s = dscr("hT_s", [8, 128, 4096], BF16)
    k.hc_s = dscr("hc_s", [1, 128, 4096], BF16)
    k.oT_s = dscr("oT_s", [8, 128, 4096], BF16)

    with ExitStack() as st:
        P = Prog(nc, st)
        k.P = P
        k.ps = [st.enter_context(nc.psum_tensor(f"ps{i}", [128, 512], F32)) for i in range(8)]

        def T(name, shape, dt):
            return st.enter_context(nc.sbuf_tensor(name, list(shape), dt))

        k.ones_f = T("ones_f", [128, 128], F32)
        k.ones_r = T("ones_r", [128, 128], F32R)
        k.ones_b = T("ones_b", [128, 128], BF16)
        k.epsT = T("epsT", [128, 1], F32)
        k.modT = T("modT", [128, 2 * 72 * 3], F32)
        k.DS = T("DS", [128, 6 * 4 * 24], F32)
        k.LNC = T("LNC", [128, 4 * 48], F32)
        k.esink = T("esink", [128, 16], F32)
        k.scb = T("scb", [128, 24], BF16)
        k.mb = T("mb", [128, 144], F32)

        phase_mod(k)
        if upto >= 1 and not _dbg("KSKIPF0"):
            phase_ffn(k, 0, WT=int(_dbg("KWT0", "512")))
        if upto >= 2:
            phase_attn(k)
        if upto >= 3:
            phase_ffn(k, 1, proj="attn", WT=512, host_mod=None if _dbg("KMODALL") else 1)
            phase_ffn(k, 2, WT=WTF)
        if upto >= 4:
            phase_ffn(k, 3, WT=WTF)
        if upto >= 5:
            phase_fourier(k)
        if upto >= 6:
            phase_ffn(k, 4, proj="four", WT=512)
            phase_ffn(k, 5, WT=WTF)
    return nc


def blkview(ap, tok0, width):
    blk, off = tok0 // 512, tok0 % 512
    assert off + width <= 512
    return ap[blk].rearrange("p (c w) -> p c w", c=8)[:, :, off:off + width]


def ds_col(s, kind, c, n):
    return ((s * 4 + kind) * 8 + c) * 3 + n


def mod_col(l, j3, c, n):
    return ((l * 9 + j3) * 8 + c) * 3 + n


def mod_groups(k, l, wb, B_w, B_psm, B_mod, B_in=()):
    P = k.P
    out = []
    for j3 in range(9):
        def dma(j3=j3):
            sl = j3 % 2
            src = k.mod_w[l][:, j3 * 1024:(j3 + 1) * 1024].rearrange("(kc p) n -> p kc n", p=128)
            P.dma("pool", wb[sl][:], src, writes=[B_w[sl]], key=f"m_w{sl}")

        def comp(j3=j3):
            sl = j3 % 2
            ps = k.ps[sl]
            for c in range(8):
                for kc in range(8):
                    MM(P, ps[:, c * 3:(c + 1) * 3], wb[sl][:, kc, c * 128:(c + 1) * 128], k.scb[:, kc * 3:(kc + 1) * 3],
                       kc == 0, kc == 7, [B_w[sl]] + list(B_in), [B_psm[sl]])
            c0 = mod_col(l, j3, 0, 0)
            o3 = k.modT[:, c0:c0 + 24].rearrange("p (c n) -> p c n", n=3)
            in0 = ps[:, 0:24].rearrange("p (c n) -> p c n", n=3)
            in1 = bc_last(k.mb[:, l * 72 + j3 * 8: l * 72 + j3 * 8 + 8], 3)
            TT(P, "dve", o3, in0, in1, ALU.add, [B_psm[sl]] + list(B_in), [B_mod])
        out.append((dma, comp))
    return out


def mod_derive(k, l, B_mod, B_in=()):
    P = k.P
    B_ds = P.buf()
    for s in range(3 * l, 3 * l + 3):
        j = s % 3
        sh = k.modT[:, mod_col(l, 3 * j, 0, 0): mod_col(l, 3 * j, 0, 0) + 24]
        sc = k.modT[:, mod_col(l, 3 * j + 1, 0, 0): mod_col(l, 3 * j + 1, 0, 0) + 24]
        gt = k.modT[:, mod_col(l, 3 * j + 2, 0, 0): mod_col(l, 3 * j + 2, 0, 0) + 24]
        A = k.DS[:, ds_col(s, 0, 0, 0): ds_col(s, 0, 0, 0) + 24]
        Bh = k.DS[:, ds_col(s, 1, 0, 0): ds_col(s, 1, 0, 0) + 24]
        G = k.DS[:, ds_col(s, 2, 0, 0): ds_col(s, 2, 0, 0) + 24]
        P.op("dve", (lambda A=A, sc=sc: lambda e: e.tensor_single_scalar(A, sc, 1.0, ALU.add))(),
             reads=[B_mod] + list(B_in), writes=[B_ds])
        if s == 0:
            P.op("dve", (lambda Bh=Bh, sh=sh: lambda e: e.tensor_copy(Bh, sh))(), reads=[B_mod], writes=[B_ds])
        else:
            ps_ = s - 1
            gp = bc_last(k.LNC[:, ps_ * 8: ps_ * 8 + 8], 3)
            bp = bc_last(k.LNC[:, 48 + ps_ * 8: 48 + ps_ * 8 + 8], 3)
            A3 = A.rearrange("p (c n) -> p c n", n=3)
            B3 = Bh.rearrange("p (c n) -> p c n", n=3)
            sh3 = sh.rearrange("p (c n) -> p c n", n=3)
            TT(P, "dve", B3, A3, bp, ALU.mult, [B_ds], [B_ds])
            TT(P, "dve", B3, B3, sh3, ALU.add, [B_ds, B_mod], [B_ds])
            TT(P, "dve", A3, A3, gp, ALU.mult, [B_ds], [B_ds])
        fac = 0.5 if j in (0, 2) else 1.0
        P.op("dve", (lambda G=G, gt=gt, fac=fac: lambda e: e.tensor_single_scalar(G, gt, fac, ALU.mult))(),
             reads=[B_mod], writes=[B_ds])


def phase_mod(k):
    P, nc = k.P, k.nc
    with ExitStack() as st:
        def T(name, shape, dt):
            return st.enter_context(nc.sbuf_tensor(name, list(shape), dt))
        cT = T("m_cT", [128, 24], F32)
        wb = [T(f"m_w{i}", [128, 8, 1024], BF16) for i in range(2)]
        sinkt = T("m_sink", [128, 16], F32)
        B_c, B_scb, B_mb, B_sink = P.buf(), P.buf(), P.buf(), P.buf()
        B_w = P.bufs_n(2)
        B_psm = P.bufs_n(2)
        B_const, B_mod, B_lnc = P.buf(), P.buf(), P.buf()

        P.dma("sp", cT[:], k.cT, writes=[B_c], key="m_c")
        P.dma("sp", k.mb[:], k.mod_bT, writes=[B_mb], key="m_mb")
        P.dma("sp", k.LNC[:, 0:48], k.lngT, writes=[B_lnc], key="m_lng")
        P.dma("sp", k.LNC[:, 48:96], k.lnbT, writes=[B_lnc], key="m_lnb")
        P.dma("sp", sinkt[:], k.sink_bc, writes=[B_sink], key="m_sink")
        P.op("dve", lambda e: e.memset(k.ones_f[:], 1.0 / D), writes=[B_const])
        P.op("dve", lambda e: e.tensor_copy(k.ones_r[:], k.ones_f[:]), reads=[B_const], writes=[B_const])
        P.op("dve", lambda e: e.memset(k.ones_b[:], 1.0), writes=[B_const])
        P.op("dve", lambda e: e.memset(k.epsT[:], LN_EPS), writes=[B_const])
        ACT(P, k.scb[:], cT[:], AF.Silu, [B_c], [B_scb])
        ACT(P, k.esink[:], sinkt[:], AF.Exp, [B_sink], [B_const])
        P.op("dve", lambda e: e.tensor_single_scalar(k.LNC[:, 96:192], k.LNC[:, 0:96], ALPHA, ALU.mult),
             reads=[B_lnc], writes=[B_lnc])
        layers = (0, 1) if _dbg("KMODALL") else (0,)
        for l in layers:
            grps = mod_groups(k, l, wb, B_w, B_psm, B_mod, B_in=[B_scb, B_mb])
            grps[0][0]()
            grps[1][0]()
            for gi in range(9):
                grps[gi][1]()
                if gi + 2 < 9:
                    grps[gi + 2][0]()
            mod_derive(k, l, B_mod, B_in=[B_lnc])
        P.barrier()
        P.flush()


WGROUPS = [(0, 4), (4, 4), (8, 4), (12, 4), (16, 4), (20, 2)]
JGRP = [gi for gi, (j0, nj) in enumerate(WGROUPS) for _ in range(nj)]


def alloc_ffn_w(k, st, tag):
    wi = st.enter_context(k.nc.sbuf_tensor(tag + "wi", [128, 8, 2 * DFF], BF16))
    wo = st.enter_context(k.nc.sbuf_tensor(tag + "wo", [128, NJ, D], BF16))
    return wi, wo


def ffn_weight_loads(k, widx, wi, wo, kp="", defer=None):
    P = k.P

    class _D:
        @staticmethod
        def dma(q, out, in_, writes, key):
            if defer is None:
                P.dma(q, out, in_, writes=writes, key=key)
            else:
                defer.append(lambda: P.dma(q, out, in_, writes=writes, key=key))
    PD = _D
    B_wa = P.bufs_n(len(WGROUPS))
    B_wg = P.bufs_n(len(WGROUPS))
    B_wo = P.bufs_n(len(WGROUPS))
    wi_src = k.ffn_wi[widx].rearrange("(kc p) n -> p kc n", p=128)
    wo_src = k.ffn_wo[widx].rearrange("(j p) n -> p j n", p=128)
    for gi, (j0, nj) in enumerate(WGROUPS):
        PD.dma("pool", wi[:, :, j0 * 128:(j0 + nj) * 128], wi_src[:, :, j0 * 128:(j0 + nj) * 128],
               writes=[B_wa[gi]], key=f"{kp}wia{gi}")
        PD.dma("pool", wi[:, :, DFF + j0 * 128:DFF + (j0 + nj) * 128],
               wi_src[:, :, DFF + j0 * 128:DFF + (j0 + nj) * 128], writes=[B_wg[gi]], key=f"{kp}wig{gi}")
    for gi, (j0, nj) in enumerate(WGROUPS):
        PD.dma("pool", wo[:, j0:j0 + nj, :], wo_src[:, j0:j0 + nj, :], writes=[B_wo[gi]], key=f"{kp}wo{gi}")
    return B_wa, B_wg, B_wo


def widx_of(s):
    return (s // 3) * 2 + (0 if s % 3 == 0 else 1)


def phase_ffn(k, s, proj=None, WT=256, host_mod=None):
    P, nc = k.P, k.nc
    l, j = s // 3, s % 3
    first = (s == 0)
    last = (s == 5)
    KJ = NJ if proj is None else 8
    nb = 3 if proj is not None else (2 if WT == 256 else 1)
    with ExitStack() as st:
        def T(name, shape, dt):
            return st.enter_context(nc.sbuf_tensor(name, list(shape), dt))
        pf = f"f{s}_"
        if proj is None:
            wi, wo = alloc_ffn_w(k, st, pf)
            B_wa, B_wg, B_wo = ffn_weight_loads(k, widx_of(s), wi, wo)
            jgrp = JGRP
            hin = [T(pf + f"hin{i}", [128, 8, WT], BF16) for i in range(nb)]
            uT = T(pf + "uT", [128, NJ, WT], BF16)
            sg = [T(pf + f"sg{i}", [128, WT], F32) for i in range(2)]
        else:
            wo = T(pf + "wo", [128, 8, D], BF16)
            uin = [T(pf + f"uin{i}", [128, 8, WT], BF16) for i in range(nb)]
            B_wo = P.bufs_n(2)
            wsrc = (k.attn_wo if proj == "attn" else k.four_wo).rearrange("(j p) n -> p j n", p=128)
            for gi in range(2):
                P.dma("pool", wo[:, gi * 4:(gi + 1) * 4, :], wsrc[:, gi * 4:(gi + 1) * 4, :], writes=[B_wo[gi]],
                      key=f"wo{gi}")
            jgrp = [0, 0, 0, 0, 1, 1, 1, 1]
        xa = [T(pf + f"xa{i}", [128, 8, WT], F32) for i in range(nb)]
        ho = T(pf + "ho", [128, 8, WT], BF16)
        if first and nb == 1:
            stage = ho[:].rearrange("p c w -> p (c w)").bitcast(F32).rearrange("p (c w) -> p c w", c=8)
        zr = T(pf + "zr", [128, WT], F32R)
        zq = T(pf + "zq", [128, WT], F32R)
        mu = T(pf + "mu", [128, WT], F32)
        msq = T(pf + "msq", [128, WT], F32)
        nzt = 1 if (proj is None and WT == 512) else 2
        zt = [T(pf + f"zt{i}", [128, WT], F32) for i in range(nzt)]
        if proj is not None:
            zqs = [T(pf + f"zqs{i}", [128, WT], F32R) for i in range(3)]
        B_mu, B_msq = P.buf(), P.buf()
        if proj is None:
            zacc, zqacc, B_zacc, B_zqacc = mu, msq, B_mu, B_msq
        else:
            zacc, zqacc = T(pf + "zacc", [128, WT], F32), T(pf + "zqacc", [128, WT], F32)
            B_zacc, B_zqacc = P.buf(), P.buf()

        B_hin = P.bufs_n(nb)
        B_xa = [P.bufs_n(8) for _ in range(nb)]
        B_u = P.bufs_n(KJ)
        B_uin = P.bufs_n(nb)
        B_sg = P.bufs_n(2)
        B_ps = P.bufs_n(8)
        B_zr, B_zq = P.buf(), P.buf()
        B_zt = P.bufs_n(nzt)
        B_zqs = P.bufs_n(3)
        B_ho = P.bufs_n(8)
        extra = []
        if host_mod is not None:
            mwb = [T(pf + f"mw{i}", [128, 8, 1024], BF16) for i in range(2)]
            B_modh = P.buf()
            grps = mod_groups(k, host_mod, mwb, P.bufs_n(2), [B_ps[0], B_ps[1]], B_modh)
            extra = [[grps[0][0], grps[1][0]]]
            for t_ in range(5):
                batch = []
                for gi in (2 * t_, 2 * t_ + 1):
                    if gi < 9:
                        batch.append(grps[gi][1])
                        if gi + 2 < 9:
                            batch.append(grps[gi + 2][0])
                extra.append(batch)
            extra.append([lambda: mod_derive(k, host_mod, B_modh)])

        tiles = [("lat", t, (t * WT) // S) for t in range(TOK // WT)]
        if first:
            tiles += [("ctx", t, 2) for t in range(CTXN // WT)]
        NT = len(tiles)

        def dview(ap, t):
            return blkview(ap, t * WT, WT)

        def load_xa(ti):
            if ti >= NT:
                return
            kind, t, n = tiles[ti]
            sl = ti % nb
            if first:
                P.dma("sp", xa[sl][:], dview(k.xT if kind == "lat" else k.ctxT, t), writes=B_xa[sl], key=f"xa{sl}")
                for c in range(8 if nb == 2 else 0):
                    a_ap = k.DS[:, ds_col(0, 0, c, n): ds_col(0, 0, c, n) + 1]
                    b_ap = k.DS[:, ds_col(0, 1, c, n): ds_col(0, 1, c, n) + 1]
                    TS(P, "pool", hin[sl][:, c, :], xa[sl][:, c, :], a_ap, b_ap, ALU.mult, ALU.add,
                       [B_xa[sl][c]], [B_hin[sl]])
                for c in range(8):
                    TS(P, "pool", xa[sl][:, c, :], xa[sl][:, c, :], ALPHA, 0.0, ALU.mult, ALU.add, [], [B_xa[sl][c]])
            else:
                P.dma("sp", xa[sl][:], dview(k.xa_s, t), writes=B_xa[sl], key=f"xa{sl}")

        def load_in(ti):
            if ti >= NT or (first and nb == 2):
                return
            kind, t, n = tiles[ti]
            if first:
                for hf in range(2):
                    src = blkview(k.xT if kind == "lat" else k.ctxT, t * WT + hf * 256, 256)
                    P.dma("sp", stage, src, writes=B_ho, key="stage")
                    for c in range(8):
                        a_ap = k.DS[:, ds_col(0, 0, c, n): ds_col(0, 0, c, n) + 1]
                        b_ap = k.DS[:, ds_col(0, 1, c, n): ds_col(0, 1, c, n) + 1]
                        TS(P, "pool", hin[0][:, c, hf * 256:(hf + 1) * 256], stage[:, c, :], a_ap, b_ap, ALU.mult, ALU.add,
                           B_ho, [B_hin[0]])
                return
            if proj is None:
                P.dma("sp", hin[ti % nb][:], dview(k.hT_s, t), writes=[B_hin[ti % nb]], key=f"hin{ti % nb}")
            else:
                P.dma("sp", uin[ti % nb][:], dview(k.oT_s, t), writes=[B_uin[ti % nb]], key=f"uin{ti % nb}")

        qe = "dve" if first else "pool"
        pending = []
        epi = []

        def run_pending():
            while pending:
                pending.pop(0)()

        def run_epi(n=1):
            for _ in range(n):
                if epi:
                    epi.pop(0)()

        for t0 in range(nb):
            load_in(t0) if not (proj is None and nb == 1 and t0 > 0) else None
            load_xa(t0)
        for ti in range(NT):
            kind, t, n = tiles[ti]
            sl = ti % nb
            if proj is None:
                for jj in range(NJ):
                    pa, pg = k.ps[jj % 2], k.ps[2 + jj % 2]
                    Ba, Bg = B_ps[jj % 2], B_ps[2 + jj % 2]
                    for kc in range(8):
                        MM(P, pa[:, 0:WT], wi[:, kc, jj * 128:(jj + 1) * 128], hin[sl][:, kc, :], kc == 0, kc == 7,
                           [B_wa[jgrp[jj]], B_hin[sl]], [Ba])
                    for kc in range(8):
                        MM(P, pg[:, 0:WT], wi[:, kc, DFF + jj * 128:DFF + (jj + 1) * 128], hin[sl][:, kc, :], kc == 0, kc == 7,
                           [B_wg[jgrp[jj]], B_hin[sl]], [Bg])
                    if jj == 1:
                        run_pending()
                    ACT(P, sg[jj % 2][:], pg[:, 0:WT], AF.Silu, [Bg], [B_sg[jj % 2]])
                    TT(P, "dve", uT[:, jj, :], pa[:, 0:WT], sg[jj % 2][:], ALU.mult, [Ba, B_sg[jj % 2]], [B_u[jj]])
                    if jj >= 1:
                        run_epi(1)
                run_epi(99)
                if nb == 1:
                    load_in(ti + 1)
                rhs_of = lambda jj: uT[:, jj, :]
                rhs_buf = lambda jj: B_u[jj]
            else:
                rhs_of = lambda jj, ti=ti: uin[ti % nb][:, jj, :]
                rhs_buf = lambda jj, ti=ti: B_uin[ti % nb]
            left0 = max(0, len(epi) - 11) if ti >= 1 else 0
            for c in range(8):
                py, By = k.ps[4 + c % 2], B_ps[4 + c % 2]
                for jj in range(KJ):
                    MM(P, py[:, 0:WT], wo[:, jj, c * 128:(c + 1) * 128], rhs_of(jj), jj == 0, jj == KJ - 1,
                       [B_wo[jgrp[jj]], rhs_buf(jj)], [By])
                if proj is not None:
                    while len(pending) > 1:
                        pending.pop(0)()
                g_ap = k.DS[:, ds_col(s, 2, c, n): ds_col(s, 2, c, n) + 1]
                zc = xa[sl][:, c, :]
                STT(P, "dve", zc, py[:, 0:WT], g_ap, zc, ALU.mult, ALU.add, [By, B_xa[sl][c]], [B_xa[sl][c]])
                if c == 0:
                    P.op("dve", (lambda zc=zc: lambda e: e.tensor_copy(zacc[:], zc))(), reads=[B_xa[sl][c]], writes=[B_zacc])
                else:
                    TT(P, "dve", zacc[:], zacc[:], zc, ALU.add, [B_zacc, B_xa[sl][c]], [B_zacc])
                if proj is not None:
                    zs, Bzs = zqs[c % 3], B_zqs[c % 3]
                    ACT(P, zs[:], zc, AF.Square, [B_xa[sl][c]], [Bzs])
                    psq, Bpsq = k.ps[2 + ti % 2], B_ps[2 + ti % 2]
                    pending.append((lambda c=c, zs=zs, Bzs=Bzs, psq=psq, Bpsq=Bpsq:
                                    lambda: MM(P, psq[:, 0:WT], k.ones_r[:], zs[:], c == 0, c == 7, [Bzs], [Bpsq]))())
                else:
                    ztc, Bztc = zt[c % nzt], B_zt[c % nzt]
                    ACT(P, ztc[:], zc, AF.Square, [B_xa[sl][c]], [Bztc])
                    if c == 0:
                        P.op(qe, (lambda ztc=ztc: lambda e: e.tensor_copy(zqacc[:], ztc[:]))(), reads=[Bztc], writes=[B_zqacc])
                    else:
                        TT(P, qe, zqacc[:], zqacc[:], ztc[:], ALU.add, [B_zqacc, Bztc], [B_zqacc])
                if proj is not None:
                    run_epi((left0, 0, 0, 1, 2, 2, 2, 2)[c])
            if extra:
                for f_ in extra.pop(0):
                    f_()
            ACT(P, zr[:], zacc[:], AF.Copy, [B_zacc], [B_zr])
            if proj is None:
                ACT(P, zq[:], zqacc[:], AF.Copy, [B_zqacc], [B_zq])
                pvar, Bpvar = k.ps[7], B_ps[7]
            else:
                pvar, Bpvar = k.ps[2 + ti % 2], B_ps[2 + ti % 2]

            def stats():
                MM(P, k.ps[6][:, 0:WT], k.ones_r[:], zr[:], True, True, [B_zr], [B_ps[6]])
                if proj is None:
                    MM(P, k.ps[7][:, 0:WT], k.ones_r[:], zq[:], True, True, [B_zq], [B_ps[7]])
            pending.append(stats)

            want_xa = not (kind == "ctx")

            def head(pvar=pvar, Bpvar=Bpvar):
                P.op("dve", lambda e: e.tensor_copy(mu[:], k.ps[6][:, 0:WT]), reads=[B_ps[6]], writes=[B_mu])
                TT(P, "dve", msq[:], mu[:], mu[:], ALU.mult, [B_mu], [B_msq])
                TT(P, "dve", msq[:], pvar[:, 0:WT], msq[:], ALU.subtract, [Bpvar, B_msq], [B_msq])
                ACT(P, msq[:], msq[:], AF.Sqrt, [B_msq], [B_msq], bias=k.epsT[:, 0:1], scale=1.0)

            def head2():
                P.op("dve", lambda e: e.reciprocal(msq[:], msq[:]), reads=[B_msq], writes=[B_msq])

            def chunk(c, sl=sl, n=n, want_xa=want_xa):
                zc = xa[sl][:, c, :]
                TT(P, "dve", zc, zc, mu[:], ALU.subtract, [B_xa[sl][c], B_mu], [B_xa[sl][c]])
                TT(P, "dve" if proj is None else "pool", zc, zc, msq[:], ALU.mult, [B_xa[sl][c], B_msq], [B_xa[sl][c]])
                lc = (l * 3 + j) * 8 + c
                if not last:
                    a_ap = k.DS[:, ds_col(s + 1, 0, c, n): ds_col(s + 1, 0, c, n) + 1]
                    b_ap = k.DS[:, ds_col(s + 1, 1, c, n): ds_col(s + 1, 1, c, n) + 1]
                    ACT(P, ho[:, c, :], zc, AF.Identity, [B_xa[sl][c]], [B_ho[c]], bias=b_ap, scale=a_ap)
                if want_xa:
                    if last:
                        ga, ba = k.LNC[:, lc:lc + 1], k.LNC[:, 48 + lc:48 + lc + 1]
                    else:
                        ga, ba = k.LNC[:, 96 + lc:96 + lc + 1], k.LNC[:, 144 + lc:144 + lc + 1]
                    if proj is None:
                        TS(P, "pool", zc, zc, ga, ba, ALU.mult, ALU.add, [B_xa[sl][c]], [B_xa[sl][c]])
                    else:
                        ACT(P, zc, zc, AF.Identity, [B_xa[sl][c]], [B_xa[sl][c]], bias=ba, scale=ga)

            def tail(ti=ti, sl=sl, kind=kind, t=t, want_xa=want_xa):
                if want_xa:
                    P.dma("sp", dview(k.yT if last else k.xa_s, t), xa[sl][:], reads=B_xa[sl], key=f"st_xo{sl}")
                if not last:
                    P.dma("sp", dview(k.hc_s if kind == "ctx" else k.hT_s, t), ho[:], reads=B_ho, key="st_ho")
                load_xa(ti + nb)
                if nb >= 2:
                    load_in(ti + nb)

            epi.append(head)
            epi.append(head2)
            for c in range(8):
                epi.append((lambda c=c, f=chunk: lambda: f(c))())
            epi.append(tail)
            if ti == NT - 1:
                run_pending()
                run_epi(99)
                while extra:
                    for f_ in extra.pop(0):
                        f_()
        P.barrier()
        P.flush()


def phase_attn(k):
    P, nc = k.P, k.nc
    with ExitStack() as st:
        def T(name, shape, dt):
            return st.enter_context(nc.sbuf_tensor(name, list(shape), dt))
        wq = T("a_wq", [128, 8, 1792], BF16)
        bqk = T("a_bqk", [128, 24], F32)
        bv = T("a_bv", [128, 256], F32)
        cosT = T("a_cos", [128, S], F32)
        sinT = T("a_sin", [128, S], F32)
        masks = T("a_masks", [128, 256], BF16)
        swm = T("a_swm", [128, 128], BF16)
        qpre = [T(f"a_qpre{i}", [128, 512], BF16) for i in range(2)]
        B_qpre = P.bufs_n(2)
        hb = [T(f"a_hb{i}", [128, 8, 512], BF16) for i in range(2)]
        hc = T("a_hc", [128, 8, 256], BF16)
        qT = T("a_qT", [128, 8, S], BF16)
        kTlo = T("a_kTlo", [128, 4, S], BF16)
        kThi = T("a_kThi", [128, 4, S], BF16)
        Vd = T("a_Vd", [128, 16, 512], BF16)
        kcTlo = T("a_kcTlo", [128, 4, 256], BF16)
        kcThi = T("a_kcThi", [128, 4, 256], BF16)
        Vc = T("a_Vc", [128, 2, 512], BF16)
        t1 = [T(f"a_t1{i}", [128, 512], F32) for i in range(2)]
        t2 = [T("a_t20", [128, 512], F32)] * 2
        PT = [T(f"a_PT{i}", [128, 512], BF16) for i in range(6)]
        R = [T(f"a_R{i}", [128, 512], F32) for i in range(2)]


        OT = [T("a_OT0", [128, 2, S], BF16)]

        B_w = [[] for _ in range(5)]
        B_c = P.buf()
        B_hb = P.bufs_n(2)
        B_hc = P.buf()
        B_q = [P.bufs_n(4) for _ in range(8)]
        B_k = [P.bufs_n(4) for _ in range(4)]
        B_V = P.bufs_n(16)
        B_kc, B_Vc = P.buf(), P.buf()
        B_t1 = P.bufs_n(2)
        B_t2 = [P.buf()] * 2
        B_PT = P.bufs_n(6)
        B_R = P.bufs_n(2)
        B_OT = P.bufs_n(1)
        B_kz = P.buf()
        B_ps = P.bufs_n(8)

        wsrc = k.wqkv.rearrange("(kc p) n -> p kc n", p=128)
        segs = [(0, 1024, 0), None, (2048, 2560, 1024), None, (3072, 3328, 1536)]
        for i, sg_ in enumerate(segs):
            if sg_ is None:
                continue
            a_, b_, o_ = sg_
            for h0 in range(a_, b_, 512):
                h1 = min(h0 + 512, b_)
                bw = P.buf()
                B_w[i].append(bw)
                P.dma("pool", wq[:, :, o_ + h0 - a_:o_ + h1 - a_], wsrc[:, :, h0:h1], writes=[bw], key=f"a_w{i}_{h0}")
        P.dma("sp", bqk[:], k.bqk, writes=[B_c], key="a_c0")
        P.dma("sp", bv[:], k.bv_bc, writes=[B_c], key="a_c1")
        P.dma("sp", cosT[:], k.cosT, writes=[B_c], key="a_c2")
        P.dma("sp", sinT[:], k.sinT, writes=[B_c], key="a_c3")
        P.dma("sp", masks[:], k.masks, writes=[B_c], key="a_c4")
        P.dma("sp", swm[:], k.SW, writes=[B_c], key="a_c5")
        P.op("pool", lambda e: e.memset(kTlo[:], 0.0), writes=[B_kz])
        P.op("pool", lambda e: e.memset(kThi[:], 0.0), writes=[B_kz])
        P.op("pool", lambda e: e.memset(kcTlo[:], 0.0), writes=[B_kz])
        P.op("pool", lambda e: e.memset(kcThi[:], 0.0), writes=[B_kz])
        QP, QS, KP, KS, VV = 0, 0, 1024, 1024, 1536
        it_rope = [0]

        rope_pending = []

        def flush_rope():
            while rope_pending:
                rope_pending.pop(0)()

        def proj_rope(dst, colp, cols, bp, bs, hbt, tt, Bw, Bdst):
            i = it_rope[0] % 2
            it_rope[0] += 1
            pA, pB = k.ps[2 * i], k.ps[2 * i + 1]
            for kc in range(8):
                MM(P, pA[:], wq[:, kc, colp:colp + 128], hbt[:, kc, :], kc == 0, kc == 7, Bw + [B_hb[tt % 2]], [B_ps[2 * i]])
            flush_rope()
            if _dbg("KT") != "2":
                ACT(P, qpre[i][:], pA[:], AF.Identity, [B_ps[2 * i], B_c], [B_qpre[i]], bias=bp, scale=1.0)
            STT(P, "dve", t1[i][:], pA[:], bp, cosT[:, tt * 512:(tt + 1) * 512], ALU.add, ALU.mult,
                [B_ps[2 * i], B_c, B_qpre[i]], [B_t1[i]])

            def rest(i=i, pB=pB, dst=dst, tt=tt, Bdst=Bdst):
                if _dbg("KT") != "1":
                    MM(P, pB[:], swm[:], qpre[i][:], True, True, [B_c, B_qpre[i]], [B_ps[2 * i + 1]])
                TT(P, "dve", t2[i][:], pB[:], sinT[:, tt * 512:(tt + 1) * 512], ALU.mult, [B_ps[2 * i + 1], B_c], [B_t2[i]])
                if isinstance(dst, list):
                    for (d_ap, hs_) in dst:
                        TT(P, "pool", d_ap, t1[i][hs_, :], t2[i][hs_, :], ALU.add, [B_t1[i], B_t2[i], B_kz], [Bdst])
                else:
                    TT(P, "pool", dst, t1[i][:], t2[i][:], ALU.add, [B_t1[i], B_t2[i]], [Bdst])
            rope_pending.append(rest)

        for b in range(2):
            P.dma("sp", hc[:], blkview(k.hc_s, b * 256, 256), writes=[B_hc], key="a_hc")
            P.dma("sp", hb[0][:], blkview(k.hT_s, b * S, 512), writes=[B_hb[0]], key="a_hb0")
            for tt in range(4):
                if tt + 1 < 4:
                    P.dma("sp", hb[(tt + 1) % 2][:], blkview(k.hT_s, b * S + (tt + 1) * 512, 512),
                          writes=[B_hb[(tt + 1) % 2]], key=f"a_hb{(tt + 1) % 2}")
                hbt = hb[tt % 2]
                for c in range(8):
                    proj_rope(qT[:, c, tt * 512:(tt + 1) * 512], QP + c * 128, QS + c * 128,
                              bqk[:, c:c + 1], bqk[:, 8 + c:9 + c], hbt, tt, B_w[0] + B_w[1], B_q[c][tt])
                for g in range(4):
                    proj_rope([(kTlo[0:64, g, tt * 512:(tt + 1) * 512], slice(0, 64)),
                               (kThi[64:128, g, tt * 512:(tt + 1) * 512], slice(64, 128))], KP + g * 128, KS + g * 128,
                              bqk[:, 16 + g:17 + g], bqk[:, 20 + g:21 + g], hbt, tt, B_w[2] + B_w[3], B_k[g][tt])
                flush_rope()
                for tb in range(4):
                    blk = tt * 4 + tb
                    pv, Bpv = k.ps[4 + tb % 2], B_ps[4 + tb % 2]
                    for kc in range(8):
                        MM(P, pv[:, 0:256], hbt[:, kc, tb * 128:(tb + 1) * 128], wq[:, kc, VV:VV + 256], kc == 0, kc == 7,
                           B_w[4] + [B_hb[tt % 2]], [Bpv])
                    o4 = Vd[:, blk, :].rearrange("p (g r d) -> p g r d", g=4, r=2)
                    i0 = pv[:, 0:256].rearrange("p (g d) -> p g d", g=4).unsqueeze(2).broadcast_to([128, 4, 2, 64])
                    i1 = bv[:, :].rearrange("p (g d) -> p g d", g=4).unsqueeze(2).broadcast_to([128, 4, 2, 64])
                    TT(P, "dve", o4, i0, i1, ALU.add, [Bpv, B_c], [B_V[blk]])
            for g in range(4):
                pk, Bpk = k.ps[6 + g % 2], B_ps[6 + g % 2]
                for kc in range(8):
                    MM(P, pk[:, 0:256], wq[:, kc, KP + g * 128:KP + (g + 1) * 128], hc[:, kc, :], kc == 0, kc == 7,
                       B_w[2] + [B_hc], [Bpk])
                ACT(P, kcTlo[0:64, g, :], pk[0:64, 0:256], AF.Identity, [Bpk, B_c, B_kz], [B_kc], bias=bqk[0:64, 16 + g:17 + g], scale=1.0)
                ACT(P, kcThi[64:128, g, :], pk[64:128, 0:256], AF.Identity, [Bpk, B_c, B_kz], [B_kc], bias=bqk[64:128, 16 + g:17 + g], scale=1.0)
            for cb in range(2):
                pv, Bpv = k.ps[4 + cb % 2], B_ps[4 + cb % 2]
                for kc in range(8):
                    MM(P, pv[:, 0:256], hc[:, kc, cb * 128:(cb + 1) * 128], wq[:, kc, VV:VV + 256], kc == 0, kc == 7,
                       B_w[4] + [B_hc], [Bpv])
                o4 = Vc[:, cb, :].rearrange("p (g r d) -> p g r d", g=4, r=2)
                i0 = pv[:, 0:256].rearrange("p (g d) -> p g d", g=4).unsqueeze(2).broadcast_to([128, 4, 2, 64])
                i1 = bv[:, :].rearrange("p (g d) -> p g d", g=4).unsqueeze(2).broadcast_to([128, 4, 2, 64])
                TT(P, "dve", o4, i0, i1, ALU.add, [Bpv, B_c], [B_Vc])
            it = 0
            norm_pending = []
            dbg = _dbg("KDBG", "")
            for g in range(0 if dbg == "a1" else (1 if dbg.startswith("a2") else 4)):
                osl = 0
                for qb in range(2 if dbg.startswith("a2") else 16):
                    tq = qb // 4
                    srcs = []
                    items = [("c", 0, None), ("c", 1, None)]
                    for d in (-1, 0, 1):
                        kb = qb + d
                        if 0 <= kb < 16:
                            items.append(("w", kb, None if d == 0 else (0 if d == -1 else 1)))
                    for (kind, kb, mk) in items:
                        slot = it % 6
                        psn = it % 2
                        it += 1
                        pss, Bpss = k.ps[psn], B_ps[psn]
                        for half in range(2):
                            if kind == "c":
                                lhsT = (kcTlo if half == 0 else kcThi)[:, g, kb * 128:(kb + 1) * 128]
                                rd = [B_kc, B_q[2 * g][tq], B_q[2 * g + 1][tq]]
                            else:
                                lhsT = (kTlo if half == 0 else kThi)[:, g, kb * 128:(kb + 1) * 128]
                                rd = [B_k[g][kb // 4], B_q[2 * g][tq], B_q[2 * g + 1][tq]]
                            MM(P, pss[:, half * 256:(half + 1) * 256].rearrange("p (c q) -> p c q", q=128), lhsT,
                               qT[:, 2 * g:2 * g + 2, qb * 128:(qb + 1) * 128], True, True, rd, [Bpss])
                        ACT(P, PT[slot][:], pss[:], AF.Exp, [Bpss], [B_PT[slot]], bias=0.0, scale=0.125)
                        if mk is not None and dbg != "a2_1":
                            m3 = bc_mid(masks[:, mk * 128:(mk + 1) * 128], 4)
                            p3 = PT[slot][:].rearrange("p (i q) -> p i q", q=128)
                            TT(P, "pool", p3, p3, m3, ALU.mult, [B_PT[slot], B_c], [B_PT[slot]])
                        vl = Vc[:, kb, g * 128:(g + 1) * 128] if kind == "c" else Vd[:, kb, g * 128:(g + 1) * 128]
                        vb = B_Vc if kind == "c" else B_V[kb]
                        srcs.append((slot, vl, vb))
                    if dbg in ("a2_1", "a2_2"):
                        continue
                    while norm_pending:
                        norm_pending.pop(0)()
                    po, Bpo = k.ps[2 + qb % 2], B_ps[2 + qb % 2]
                    pm, Bpm = k.ps[4 + qb % 2], B_ps[4 + qb % 2]
                    for ii, (slot, vl, vb) in enumerate(srcs):
                        MM(P, po[:], vl, PT[slot][:], ii == 0, ii == len(srcs) - 1, [vb, B_PT[slot]], [Bpo])
                        MM(P, pm[:], k.ones_b[:], PT[slot][:], ii == 0, ii == len(srcs) - 1, [B_PT[slot]], [Bpm])

                    def norm(g=g, qb=qb, po=po, Bpo=Bpo, pm=pm, Bpm=Bpm, osl=osl):
                        r = R[qb % 2]
                        use_act = True
                        for cb in range(4):
                            i = 2 * (cb % 2) + cb // 2
                            es = k.esink[:, 4 * g + i:4 * g + i + 1]
                            if use_act:
                                ACT(P, r[:, cb * 128:(cb + 1) * 128], pm[:, cb * 128:(cb + 1) * 128], AF.Ln, [Bpm], [B_R[qb % 2]],
                                    bias=es, scale=1.0)
                            else:
                                P.op("dve", (lambda o=r[:, cb * 128:(cb + 1) * 128], a=pm[:, cb * 128:(cb + 1) * 128], es=es:
                                             lambda e: e.tensor_single_scalar(o, a, es, ALU.add))(), reads=[Bpm], writes=[B_R[qb % 2]])
                        if use_act:
                            ACT(P, r[:], r[:], AF.Exp, [B_R[qb % 2]], [B_R[qb % 2]], bias=0.0, scale=-1.0)
                        else:
                            P.op("dve", (lambda r=r: lambda e: e.reciprocal(r[:], r[:]))(), reads=[B_R[qb % 2]], writes=[B_R[qb % 2]])
                        for cb in range(4):
                            i = 2 * (cb % 2) + cb // 2
                            half = i % 2
                            hs = slice(half * 64, (half + 1) * 64)
                            TT(P, "dve", OT[osl][hs, i // 2, qb * 128:(qb + 1) * 128], po[hs, cb * 128:(cb + 1) * 128],
                               r[hs, cb * 128:(cb + 1) * 128], ALU.mult, [Bpo, B_R[qb % 2]], [B_OT[osl]])
                    norm_pending.append(norm)
                while norm_pending:
                    norm_pending.pop(0)()
                for tt in range(4):
                    dst = blkview(k.oT_s, b * S + tt * 512, 512)[:, 2 * g:2 * g + 2, :]
                    P.dma("sp", dst, OT[osl][:, :, tt * 512:(tt + 1) * 512], reads=[B_OT[osl]], key=f"a_ot{tt}")
        P.barrier()
        P.flush()


def phase_fourier(k):
    P, nc = k.P, k.nc
    with ExitStack() as st:
        def T(name, shape, dt):
            return st.enter_context(nc.sbuf_tensor(name, list(shape), dt))
        csc = T("u_csc", [128, 2, 512], BF16)
        fm = T("u_fm", [128, 5, 128], BF16)
        hb = [T(f"u_hb{i}", [128, 8, 512], BF16) for i in range(2)]
        A = T("u_A", [128, 16, D], BF16)
        Bm = T("u_B", [128, 16, D], BF16)
        Az = T("u_Az", [128, D], BF16)
        ctab = [T(f"u_ct{i}", [128, 9, 512], BF16) for i in range(2)]
        stab = [T(f"u_st{i}", [128, 8, 512], BF16) for i in range(2)]
        fo = [T(f"u_fo{i}", [128, 512], BF16) for i in range(2)]
        B_csc = P.buf()
        B_hb = P.bufs_n(2)
        B_A = P.bufs_n(16)
        B_Az = P.buf()
        B_tab = P.bufs_n(2)
        B_fo = P.bufs_n(2)
        B_ps = P.bufs_n(8)
        P.dma("sp", csc[:], k.CSC.rearrange("(kc p) n -> p kc n", p=128), writes=[B_csc], key="u_csc")
        P.dma("sp", fm[:], k.FM.rearrange("p (j m) -> p j m", j=5), writes=[B_csc], key="u_fm")
        clsrc = k.CL.rearrange("(lb p) n -> p lb n", p=128)
        slsrc = k.SLn.rearrange("(lb p) n -> p lb n", p=128)
        ti = 0
        fi = 0
        ei = 0
        for b in range(2):
            def load_tab(kt, sl):
                P.dma("sp", ctab[sl][:], clsrc[:, 0:9, kt * 512:(kt + 1) * 512], writes=[B_tab[sl]], key=f"u_ct{sl}")
                P.dma("sp", stab[sl][:], slsrc[:, 0:8, kt * 512:(kt + 1) * 512], writes=[B_tab[sl]], key=f"u_st{sl}")
            load_tab(0, ti % 2)
            P.dma("sp", hb[0][:], blkview(k.hT_s, b * S, 512), writes=[B_hb[0]], key="u_hb0")
            for tt in range(4):
                if tt + 1 < 4:
                    P.dma("sp", hb[(tt + 1) % 2][:], blkview(k.hT_s, b * S + (tt + 1) * 512, 512),
                          writes=[B_hb[(tt + 1) % 2]], key=f"u_hb{(tt + 1) % 2}")
                for tb in range(4):
                    blk = tt * 4 + tb
                    for gq in range(4):
                        pi = (blk * 4 + gq) % 2
                        pp, Bpp = k.ps[pi], B_ps[pi]
                        for kc in range(2):
                            MM(P, pp[:], hb[tt % 2][:, gq * 2 + kc, tb * 128:(tb + 1) * 128], csc[:, kc, :], kc == 0, kc == 1,
                               [B_hb[tt % 2], B_csc], [Bpp])
                        ACT(P, A[:, blk, gq * 256:(gq + 1) * 256], pp[:, 0:256], AF.Copy, [Bpp], [B_A[blk]])
                        P.op("dve", (lambda o=Bm[:, blk, gq * 256:(gq + 1) * 256], i=pp[:, 256:512]:
                                     lambda e: e.tensor_copy(o, i))(), reads=[Bpp], writes=[B_A[blk]])
            def evac(out_ap, pp, Bpp, Bout):
                nonlocal ei
                if ei % 2 == 0:
                    ACT(P, out_ap, pp[:], AF.Copy, [Bpp], [Bout])
                else:
                    P.op("dve", (lambda o=out_ap, i=pp[:]: lambda e: e.tensor_copy(o, i))(), reads=[Bpp], writes=[Bout])
                ei += 1
            for hf in range(2):
                pp, Bpp = k.ps[4 + hf], B_ps[4 + hf]
                MM(P, pp[:], fm[:, 2, :], A[:, 8, hf * 512:(hf + 1) * 512], True, True, [B_csc, B_A[8]], [Bpp])
                evac(Az[:, hf * 512:(hf + 1) * 512], pp, Bpp, B_Az)
            for lb in range(8):
                for hf in range(2):
                    cs = slice(hf * 512, (hf + 1) * 512)
                    for (X, jn, en) in ((A, 1, 2), (Bm, 3, 4)):
                        pi = 4 + ei % 2
                        pp, Bpp = k.ps[pi], B_ps[pi]
                        MM(P, pp[:], fm[:, 0, :], X[:, lb, cs], True, False, [B_csc, B_A[lb]], [Bpp])
                        MM(P, pp[:], fm[:, jn, :], X[:, 15 - lb, cs], False, lb == 0, [B_csc, B_A[15 - lb]], [Bpp])
                        if lb >= 1:
                            MM(P, pp[:], fm[:, en, :], X[:, 16 - lb, cs], False, True, [B_csc, B_A[16 - lb]], [Bpp])
                        evac(X[:, lb, cs], pp, Bpp, B_A[lb])
            for kt in range(4):
                sl = ti % 2
                ti += 1
                if kt + 1 < 4:
                    load_tab(kt + 1, ti % 2)
                for cc in range(8):
                    pi = 2 + fi % 2
                    pp, Bpp = k.ps[pi], B_ps[pi]
                    for lb in range(8):
                        MM(P, pp[:], A[:, lb, cc * 128:(cc + 1) * 128], ctab[sl][:, lb, :], lb == 0, False,
                           [B_A[lb], B_tab[sl]], [Bpp])
                        MM(P, pp[:], Bm[:, lb, cc * 128:(cc + 1) * 128], stab[sl][:, lb, :], False, False,
                           [B_A[lb], B_tab[sl]], [Bpp])
                    MM(P, pp[:], Az[:, cc * 128:(cc + 1) * 128], ctab[sl][:, 8, :], False, True, [B_Az, B_tab[sl]], [Bpp])
                    fs = fi % 2
                    fi += 1
                    if fs == 0:
                        ACT(P, fo[fs][:], pp[:], AF.Copy, [Bpp], [B_fo[fs]])
                    else:
                        P.op("dve", (lambda o=fo[fs][:], i=pp[:]: lambda e: e.tensor_copy(o, i))(), reads=[Bpp], writes=[B_fo[fs]])
                    P.dma("sp", blkview(k.oT_s, b * S + kt * 512, 512)[:, cc, :], fo[fs][:], reads=[B_fo[fs]], key=f"u_fo{fs}")
        P.barrier()
        P.flush()


def _const_tables():
    bf = ml_dtypes.bfloat16
    t = np.arange(S)
    row = (t // 64).astype(np.float64)
    col = (t % 64).astype(np.float64)
    inv = 10000.0 ** (-np.arange(0, 32, 2, dtype=np.float64) / 32.0)
    cosT = np.zeros((128, S), np.float64)
    sinT = np.zeros((128, S), np.float64)
    for p in range(128):
        r = p % 64
        part, axis, i = r // 32, (r % 32) // 16, r % 16
        ang = (row if axis == 0 else col) * inv[i]
        cosT[p] = np.cos(ang)
        sinT[p] = np.sin(ang) * (-1.0 if part == 0 else 1.0)
    kk = np.arange(128)
    m_prev = (kk[None, :] <= kk[:, None]).astype(np.float32)
    m_next = (kk[:, None] <= kk[None, :]).astype(np.float32)
    masks = np.concatenate([m_prev, m_next], axis=1).astype(bf)
    ll = np.arange(S, dtype=np.float64)
    angL = 2.0 * np.pi * ((ll[:, None] * ll[None, :]) % S) / S
    CL = (np.cos(angL) / np.sqrt(S)).astype(np.float32).astype(bf)
    SLn = (-np.sin(angL) / np.sqrt(S)).astype(np.float32).astype(bf)
    cc = np.arange(256, dtype=np.float64)
    angC = 2.0 * np.pi * ((cc[:, None] * cc[None, :]) % 256) / 256
    CSC = np.concatenate([np.cos(angC) / 16.0, np.sin(angC) / 16.0], axis=1).astype(np.float32).astype(bf)
    eye = np.eye(128, dtype=np.float32)
    J1 = np.zeros((128, 128), np.float32)
    for m in range(1, 128):
        J1[128 - m, m] = 1.0
    E0 = np.zeros((128, 128), np.float32)
    E0[0, 0] = 1.0
    FM = np.concatenate([eye, J1, E0, -J1, -E0], axis=1).astype(bf)
    SW = np.zeros((128, 128), np.float32)
    for m in range(128):
        SW[m ^ 32, m] = 1.0
    SW = SW.astype(bf)
    return cosT.astype(np.float32), sinT.astype(np.float32), masks, CL, SLn, CSC, FM, SW


def _head_perm():
    perm = np.zeros(64, np.int64)
    swp = np.zeros(64, np.int64)
    for r in range(64):
        part, axis, i = r // 32, (r % 32) // 16, r % 16
        perm[r] = axis * 32 + part * 16 + i
        swp[r] = axis * 32 + (1 - part) * 16 + i
    return perm, swp


def prepare_inputs(x, c, ctx, c_ctx, mod_w, mod_b, ln_g, ln_b, ffn_wi, ffn_wo,
                   attn_wqkv, attn_bqkv, attn_wo, attn_sink, fourier_wo):
    f32 = np.float32
    x = np.asarray(x, f32); c = np.asarray(c, f32); ctx = np.asarray(ctx, f32); c_ctx = np.asarray(c_ctx, f32)
    cosT, sinT, masks, CL, SLn, CSC, FM, SW = _const_tables()
    perm, swp = _head_perm()
    wqkv = np.asarray(attn_wqkv, f32)[0]
    bqkv = np.asarray(attn_bqkv, f32)[0]
    qcols_p = np.concatenate([h * 64 + perm for h in range(16)])
    qcols_s = np.concatenate([h * 64 + swp for h in range(16)])
    kcols_p = np.concatenate([np.concatenate([1024 + g * 64 + perm] * 2) for g in range(4)])
    kcols_s = np.concatenate([np.concatenate([1024 + g * 64 + swp] * 2) for g in range(4)])
    vcols = 1280 + np.arange(256)
    allc = np.concatenate([qcols_p, qcols_s, kcols_p, kcols_s, vcols])
    wq_x = np.ascontiguousarray(wqkv[:, allc])
    bq_p = bqkv[qcols_p].reshape(8, 128).T
    bq_s = bqkv[qcols_s].reshape(8, 128).T
    bk_p = bqkv[kcols_p].reshape(4, 128).T
    bk_s = bqkv[kcols_s].reshape(4, 128).T
    bqk = np.ascontiguousarray(np.concatenate([bq_p, bq_s, bk_p, bk_s], axis=1))
    bv_bc = np.ascontiguousarray(np.broadcast_to(bqkv[vcols][None, :], (128, 256)))
    sink_bc = np.ascontiguousarray(np.broadcast_to(np.asarray(attn_sink, f32)[0][None, :], (128, 16)))
    mod_bT = np.ascontiguousarray(np.concatenate([np.asarray(mod_b, f32)[l].reshape(72, 128).T for l in range(2)], axis=1))
    lngT = np.ascontiguousarray(np.asarray(ln_g, f32).reshape(48, 128).T)
    lnbT = np.ascontiguousarray(np.asarray(ln_b, f32).reshape(48, 128).T)
    shared = dict(
        mod_w=np.ascontiguousarray(np.asarray(mod_w, f32)), mod_bT=mod_bT, lngT=lngT, lnbT=lnbT,
        ffn_wi=np.ascontiguousarray(np.asarray(ffn_wi, f32).reshape(4, D, 2 * DFF)),
        ffn_wo=np.ascontiguousarray(np.asarray(ffn_wo, f32).reshape(4, DFF, D)),
        wqkv=wq_x, bqk=bqk, bv_bc=bv_bc, attn_wo=np.ascontiguousarray(np.asarray(attn_wo, f32)[0]), sink_bc=sink_bc,
        four_wo=np.ascontiguousarray(np.asarray(fourier_wo, f32)[0]),
        cosT=cosT, sinT=sinT, masks=masks, CL=CL, SLn=SLn, CSC=CSC, FM=FM, SW=SW,
    )
    in_maps = []
    for core in range(NCORES):
        b0 = 2 * core
        xT = np.ascontiguousarray(x[b0:b0 + 2].reshape(8, 512, 8, 128).transpose(0, 3, 2, 1)).reshape(8, 128, 4096)
        ctxT = np.ascontiguousarray(ctx[b0:b0 + 2].reshape(1, 512, 8, 128).transpose(0, 3, 2, 1)).reshape(1, 128, 4096)
        cv = np.stack([c[b0], c[b0 + 1], c_ctx], axis=1)
        cT = np.ascontiguousarray(cv.reshape(8, 128, 3).transpose(1, 0, 2).reshape(128, 24))
        m = dict(shared)
        m.update(xT=xT, ctxT=ctxT, cT=cT)
        in_maps.append(m)
    return in_maps


def unblock(yT):
    return np.asarray(yT).reshape(8, 128, 8, 512).transpose(0, 3, 2, 1).reshape(2, S, D)


_NC_CACHE = {}


def kernel(x, c, ctx, c_ctx, mod_w, mod_b, ln_g, ln_b, ffn_wi, ffn_wo,
           attn_wqkv, attn_bqkv, attn_wo, attn_sink, fourier_wo):
    in_maps = prepare_inputs(x, c, ctx, c_ctx, mod_w, mod_b, ln_g, ln_b, ffn_wi, ffn_wo,
                             attn_wqkv, attn_bqkv, attn_wo, attn_sink, fourier_wo)
    if "nc" not in _NC_CACHE:
        _NC_CACHE["nc"] = build()
    nc = _NC_CACHE["nc"]
    res = run_bass_kernel_spmd(nc, in_maps, core_ids=list(range(NCORES)))
    out = np.empty((16, S, D), np.float32)
    for core in range(NCORES):
        yT = res.results[core]["yT"]
        out[2 * core:2 * core + 2] = unblock(yT)
    return out
```

```python
import os
import numpy as np
import ml_dtypes
from contextlib import ExitStack
import concourse.bass as bass
import concourse.mybir as mybir
from concourse.bass_utils import run_bass_kernel_spmd

F32 = mybir.dt.float32
F32R = mybir.dt.float32r
BF16 = mybir.dt.bfloat16
ALU = mybir.AluOpType
AF = mybir.ActivationFunctionType

ENGS = ("pe", "act", "dve", "pool", "sp")
NCORES = 8
D = 1024
S = 2048
TOK = 4096
CTXN = 512
DFF = 2816
NJ = 22
W = 256
WTF = 512


def _dbg(name, default=None):
    if os.environ.get("KERNEL_DEBUG_HOOKS") != "1":
        return default
    return os.environ.get(name, default)
ALPHA = 4.0 ** 0.25
LN_EPS = 1e-5


class Buf:
    __slots__ = ("name", "last_w", "readers")

    def __init__(self, name):
        self.name = name
        self.last_w = None
        self.readers = []


class Op:
    __slots__ = ("eng", "fn", "deps", "is_dma", "key", "needs_signal", "tok", "kind", "name")

    def __init__(self, eng, fn, is_dma=False, key=None, kind="op", name=""):
        self.eng = eng
        self.fn = fn
        self.deps = []
        self.is_dma = is_dma
        self.key = key
        self.needs_signal = False
        self.tok = None
        self.kind = kind
        self.name = name


class Prog:
    def __init__(self, nc, stack):
        self.nc = nc
        self.stack = stack
        self.ops = {e: [] for e in ENGS}
        self.bufs = []
        self.sem = {}
        self.cnt = {}
        self.waited = {e: {} for e in ENGS}
        self.epoch_dma = {}
        for e in ("pe", "act", "dve", "pool"):
            self._mksem("prog_" + e)
        self._mksem("bar")

    def _mksem(self, name):
        s = self.stack.enter_context(self.nc.semaphore(name))
        self.sem[name] = s
        self.cnt[name] = 0
        return s

    def buf(self, name=""):
        b = Buf(name)
        self.bufs.append(b)
        return b

    def bufs_n(self, n, name=""):
        return [self.buf(f"{name}{i}") for i in range(n)]

    def _record(self, o, reads, writes):
        deps = []
        for b in reads:
            if b.last_w is not None:
                deps.append(b.last_w)
        for b in writes:
            if b.last_w is not None:
                deps.append(b.last_w)
            deps.extend(b.readers)
        seen = set()
        for d in deps:
            if d is o or id(d) in seen:
                continue
            seen.add(id(d))
            if (not d.is_dma) and (not o.is_dma) and d.eng == "pe" and o.eng == "pe":
                continue
            o.deps.append(d)
            d.needs_signal = True
        for b in reads:
            b.readers.append(o)
        for b in writes:
            b.last_w = o
            b.readers = []
        self.ops[o.eng].append(o)
        return o

    def op(self, eng, fn, reads=(), writes=(), name=""):
        return self._record(Op(eng, fn, name=name), reads, writes)

    def dma(self, queue, out, in_, reads=(), writes=(), key=None):
        k = "dma_" + key
        if k not in self.sem:
            self._mksem(k)
        o = Op(queue, lambda e: e.dma_start(out=out, in_=in_), is_dma=True, key=k)
        self._record(o, reads, writes)
        o.needs_signal = True
        self.epoch_dma[k] = o
        return o

    def barrier(self):
        b = Op("sp", None, kind="barrier")
        for e in ("pe", "act", "dve", "pool"):
            for o in reversed(self.ops[e]):
                if o.kind == "op" and not o.is_dma:
                    o.needs_signal = True
                    b.deps.append(o)
                    break
        for o in self.epoch_dma.values():
            b.deps.append(o)
        self.epoch_dma = {}
        self.ops["sp"].append(b)
        for e in ("pe", "act", "dve", "pool"):
            w = Op(e, None, kind="barwait")
            w.deps.append(b)
            self.ops[e].append(w)
        for bf in self.bufs:
            bf.last_w = None
            bf.readers = []
        self.bufs = []

    def _assign(self):
        for e in ENGS:
            for o in self.ops[e]:
                if o.tok is not None:
                    continue
                if o.kind == "barrier":
                    self.cnt["bar"] += 1
                    o.tok = ("bar", self.cnt["bar"])
                elif o.kind == "barwait":
                    o.tok = ("none", 0)
                elif o.is_dma:
                    self.cnt[o.key] += 16
                    o.tok = (o.key, self.cnt[o.key])
                elif o.needs_signal:
                    k = "prog_" + e
                    self.cnt[k] += 1
                    o.tok = (k, self.cnt[k])
                else:
                    o.tok = ("none", 0)

    def _emit_eng(self, e, eng):
        waited = self.waited[e]
        for o in self.ops[e]:
            need = {}
            for d in o.deps:
                k, v = d.tok
                assert k != "none", (o.name, d.name)
                if v > need.get(k, 0):
                    need[k] = v
            for k, v in need.items():
                if waited.get(k, 0) >= v:
                    continue
                eng.wait_ge(self.sem[k], v)
                waited[k] = v
            if o.kind == "barrier":
                eng.sem_inc(self.sem["bar"], 1)
                continue
            if o.kind == "barwait":
                continue
            ins = o.fn(eng)
            k, v = o.tok
            if k != "none":
                ins.then_inc(self.sem[k], 16 if o.is_dma else 1)

    def flush(self, name=None):
        self._assign()
        self.nflush = getattr(self, "nflush", 0) + 1
        with self.nc.named_scope(name or f"ph{self.nflush}"), self.nc.Block() as block:
            @block.tensor
            def _(eng):
                self._emit_eng("pe", eng)

            @block.scalar
            def _(eng):
                self._emit_eng("act", eng)

            @block.vector
            def _(eng):
                self._emit_eng("dve", eng)

            @block.gpsimd
            def _(eng):
                self._emit_eng("pool", eng)

            @block.sync
            def _(eng):
                self._emit_eng("sp", eng)
        self.ops = {e: [] for e in ENGS}


def MM(P, out, lhsT, rhs, start, stop, reads, writes):
    return P.op("pe", lambda e: e.matmul(out, lhsT, rhs, start=start, stop=stop), reads, writes)


def ACT(P, out, in_, func, reads, writes, bias=0.0, scale=1.0):
    return P.op("act", lambda e: e.activation(out, in_, func, bias=bias, scale=scale), reads, writes)


def TT(P, eng, out, in0, in1, op, reads, writes):
    return P.op(eng, lambda e: e.tensor_tensor(out, in0, in1, op), reads, writes)


def TS(P, eng, out, in0, s1, s2, op0, op1, reads, writes):
    return P.op(eng, lambda e: e.tensor_scalar(out, in0, s1, s2, op0, op1), reads, writes)


def STT(P, eng, out, in0, scalar, in1, op0, op1, reads, writes):
    return P.op(eng, lambda e: e.scalar_tensor_tensor(out, in0, scalar, in1, op0, op1), reads, writes)


def bc_mid(ap2d, n):
    return ap2d.unsqueeze(1).broadcast_to([ap2d.shape[0], n, ap2d.shape[1]])


def bc_last(ap2d, n):
    return ap2d.unsqueeze(2).broadcast_to([ap2d.shape[0], ap2d.shape[1], n])


class K:
    pass


def build(upto=99, debug=False):
    nc = bass.Bass("TRN2", target_bir_lowering=False)
    k = K()
    k.nc = nc

    def din(name, shape, dt=F32):
        return nc.dram_tensor(name, list(shape), dt, kind="ExternalInput").ap()

    skind = "ExternalOutput" if debug else "Internal"

    def dscr(name, shape, dt):
        return nc.dram_tensor(name, list(shape), dt, kind=skind).ap()

    k.xT = din("xT", [8, 128, 4096])
    k.ctxT = din("ctxT", [1, 128, 4096])
    k.cT = din("cT", [128, 24])
    k.mod_w = din("mod_w", [2, D, 9 * D])
    k.mod_bT = din("mod_bT", [128, 144])
    k.lngT = din("lngT", [128, 48])
    k.lnbT = din("lnbT", [128, 48])
    k.ffn_wi = din("ffn_wi", [4, D, 2 * DFF])
    k.ffn_wo = din("ffn_wo", [4, DFF, D])
    k.wqkv = din("wqkv", [D, 3328])
    k.bqk = din("bqk", [128, 24])
    k.bv_bc = din("bv_bc", [128, 256])
    k.attn_wo = din("attn_wo", [D, D])
    k.sink_bc = din("sink_bc", [128, 16])
    k.four_wo = din("four_wo", [D, D])
    k.cosT = din("cosT", [128, S])
    k.sinT = din("sinT", [128, S])
    k.masks = din("masks", [128, 256], BF16)
    k.CL = din("CL", [S, S], BF16)
    k.SLn = din("SLn", [S, S], BF16)
    k.CSC = din("CSC", [256, 512], BF16)
    k.FM = din("FM", [128, 640], BF16)
    k.SW = din("SW", [128, 128], BF16)
    k.yT = nc.dram_tensor("yT", [8, 128, 4096], F32, kind="ExternalOutput").ap()
    k.xa_s = dscr("xa_s", [8, 128, 4096], F32)
    k.hT_s = dscr("hT_s", [8, 128, 4096], BF16)
    k.hc_s = dscr("hc_s", [1, 128, 4096], BF16)
    k.oT_s = dscr("oT_s", [8, 128, 4096], BF16)

    with ExitStack() as st:
        P = Prog(nc, st)
        k.P = P
        k.ps = [st.enter_context(nc.psum_tensor(f"ps{i}", [128, 512], F32)) for i in range(8)]

        def T(name, shape, dt):
            return st.enter_context(nc.sbuf_tensor(name, list(shape), dt))

        k.ones_f = T("ones_f", [128, 128], F32)
        k.ones_r = T("ones_r", [128, 128], F32R)
        k.ones_b = T("ones_b", [128, 128], BF16)
        k.epsT = T("epsT", [128, 1], F32)
        k.modT = T("modT", [128, 2 * 72 * 3], F32)
        k.DS = T("DS", [128, 6 * 4 * 24], F32)
        k.LNC = T("LNC", [128, 4 * 48], F32)
        k.esink = T("esink", [128, 16], F32)
        k.scb = T("scb", [128, 24], BF16)
        k.mb = T("mb", [128, 144], F32)

        phase_mod(k)
        if upto >= 1 and not _dbg("KSKIPF0"):
            phase_ffn(k, 0, WT=int(_dbg("KWT0", "512")))
        if upto >= 2:
            phase_attn(k)
        if upto >= 3:
            phase_ffn(k, 1, proj="attn", WT=512, host_mod=None if _dbg("KMODALL") else 1)
            phase_ffn(k, 2, WT=WTF)
        if upto >= 4:
            phase_ffn(k, 3, WT=WTF)
        if upto >= 5:
            phase_fourier(k)
        if upto >= 6:
            phase_ffn(k, 4, proj="four", WT=512)
            phase_ffn(k, 5, WT=WTF)
    return nc


def blkview(ap, tok0, width):
    blk, off = tok0 // 512, tok0 % 512
    assert off + width <= 512
    return ap[blk].rearrange("p (c w) -> p c w", c=8)[:, :, off:off + width]


def ds_col(s, kind, c, n):
    return ((s * 4 + kind) * 8 + c) * 3 + n


def mod_col(l, j3, c, n):
    return ((l * 9 + j3) * 8 + c) * 3 + n


def mod_groups(k, l, wb, B_w, B_psm, B_mod, B_in=()):
    P = k.P
    out = []
    for j3 in range(9):
        def dma(j3=j3):
            sl = j3 % 2
            src = k.mod_w[l][:, j3 * 1024:(j3 + 1) * 1024].rearrange("(kc p) n -> p kc n", p=128)
            P.dma("pool", wb[sl][:], src, writes=[B_w[sl]], key=f"m_w{sl}")

        def comp(j3=j3):
            sl = j3 % 2
            ps = k.ps[sl]
            for c in range(8):
                for kc in range(8):
                    MM(P, ps[:, c * 3:(c + 1) * 3], wb[sl][:, kc, c * 128:(c + 1) * 128], k.scb[:, kc * 3:(kc + 1) * 3],
                       kc == 0, kc == 7, [B_w[sl]] + list(B_in), [B_psm[sl]])
            c0 = mod_col(l, j3, 0, 0)
            o3 = k.modT[:, c0:c0 + 24].rearrange("p (c n) -> p c n", n=3)
            in0 = ps[:, 0:24].rearrange("p (c n) -> p c n", n=3)
            in1 = bc_last(k.mb[:, l * 72 + j3 * 8: l * 72 + j3 * 8 + 8], 3)
            TT(P, "dve", o3, in0, in1, ALU.add, [B_psm[sl]] + list(B_in), [B_mod])
        out.append((dma, comp))
    return out


def mod_derive(k, l, B_mod, B_in=()):
    P = k.P
    B_ds = P.buf()
    for s in range(3 * l, 3 * l + 3):
        j = s % 3
        sh = k.modT[:, mod_col(l, 3 * j, 0, 0): mod_col(l, 3 * j, 0, 0) + 24]
        sc = k.modT[:, mod_col(l, 3 * j + 1, 0, 0): mod_col(l, 3 * j + 1, 0, 0) + 24]
        gt = k.modT[:, mod_col(l, 3 * j + 2, 0, 0): mod_col(l, 3 * j + 2, 0, 0) + 24]
        A = k.DS[:, ds_col(s, 0, 0, 0): ds_col(s, 0, 0, 0) + 24]
        Bh = k.DS[:, ds_col(s, 1, 0, 0): ds_col(s, 1, 0, 0) + 24]
        G = k.DS[:, ds_col(s, 2, 0, 0): ds_col(s, 2, 0, 0) + 24]
        P.op("dve", (lambda A=A, sc=sc: lambda e: e.tensor_single_scalar(A, sc, 1.0, ALU.add))(),
             reads=[B_mod] + list(B_in), writes=[B_ds])
        if s == 0:
            P.op("dve", (lambda Bh=Bh, sh=sh: lambda e: e.tensor_copy(Bh, sh))(), reads=[B_mod], writes=[B_ds])
        else:
            ps_ = s - 1
            gp = bc_last(k.LNC[:, ps_ * 8: ps_ * 8 + 8], 3)
            bp = bc_last(k.LNC[:, 48 + ps_ * 8: 48 + ps_ * 8 + 8], 3)
            A3 = A.rearrange("p (c n) -> p c n", n=3)
            B3 = Bh.rearrange("p (c n) -> p c n", n=3)
            sh3 = sh.rearrange("p (c n) -> p c n", n=3)
            TT(P, "dve", B3, A3, bp, ALU.mult, [B_ds], [B_ds])
            TT(P, "dve", B3, B3, sh3, ALU.add, [B_ds, B_mod], [B_ds])
            TT(P, "dve", A3, A3, gp, ALU.mult, [B_ds], [B_ds])
        fac = 0.5 if j in (0, 2) else 1.0
        P.op("dve", (lambda G=G, gt=gt, fac=fac: lambda e: e.tensor_single_scalar(G, gt, fac, ALU.mult))(),
             reads=[B_mod], writes=[B_ds])


def phase_mod(k):
    P, nc = k.P, k.nc
    with ExitStack() as st:
        def T(name, shape, dt):
            return st.enter_context(nc.sbuf_tensor(name, list(shape), dt))
        cT = T("m_cT", [128, 24], F32)
        wb = [T(f"m_w{i}", [128, 8, 1024], BF16) for i in range(2)]
        sinkt = T("m_sink", [128, 16], F32)
        B_c, B_scb, B_mb, B_sink = P.buf(), P.buf(), P.buf(), P.buf()
        B_w = P.bufs_n(2)
        B_psm = P.bufs_n(2)
        B_const, B_mod, B_lnc = P.buf(), P.buf(), P.buf()

        P.dma("sp", cT[:], k.cT, writes=[B_c], key="m_c")
        P.dma("sp", k.mb[:], k.mod_bT, writes=[B_mb], key="m_mb")
        P.dma("sp", k.LNC[:, 0:48], k.lngT, writes=[B_lnc], key="m_lng")
        P.dma("sp", k.LNC[:, 48:96], k.lnbT, writes=[B_lnc], key="m_lnb")
        P.dma("sp", sinkt[:], k.sink_bc, writes=[B_sink], key="m_sink")
        P.op("dve", lambda e: e.memset(k.ones_f[:], 1.0 / D), writes=[B_const])
        P.op("dve", lambda e: e.tensor_copy(k.ones_r[:], k.ones_f[:]), reads=[B_const], writes=[B_const])
        P.op("dve", lambda e: e.memset(k.ones_b[:], 1.0), writes=[B_const])
        P.op("dve", lambda e: e.memset(k.epsT[:], LN_EPS), writes=[B_const])
        ACT(P, k.scb[:], cT[:], AF.Silu, [B_c], [B_scb])
        ACT(P, k.esink[:], sinkt[:], AF.Exp, [B_sink], [B_const])
        P.op("dve", lambda e: e.tensor_single_scalar(k.LNC[:, 96:192], k.LNC[:, 0:96], ALPHA, ALU.mult),
             reads=[B_lnc], writes=[B_lnc])
        layers = (0, 1) if _dbg("KMODALL") else (0,)
        for l in layers:
            grps = mod_groups(k, l, wb, B_w, B_psm, B_mod, B_in=[B_scb, B_mb])
            grps[0][0]()
            grps[1][0]()
            for gi in range(9):
                grps[gi][1]()
                if gi + 2 < 9:
                    grps[gi + 2][0]()
            mod_derive(k, l, B_mod, B_in=[B_lnc])
        P.barrier()
        P.flush()


WGROUPS = [(0, 4), (4, 4), (8, 4), (12, 4), (16, 4), (20, 2)]
JGRP = [gi for gi, (j0, nj) in enumerate(WGROUPS) for _ in range(nj)]


def alloc_ffn_w(k, st, tag):
    wi = st.enter_context(k.nc.sbuf_tensor(tag + "wi", [128, 8, 2 * DFF], BF16))
    wo = st.enter_context(k.nc.sbuf_tensor(tag + "wo", [128, NJ, D], BF16))
    return wi, wo


def ffn_weight_loads(k, widx, wi, wo, kp="", defer=None):
    P = k.P

    class _D:
        @staticmethod
        def dma(q, out, in_, writes, key):
            if defer is None:
                P.dma(q, out, in_, writes=writes, key=key)
            else:
                defer.append(lambda: P.dma(q, out, in_, writes=writes, key=key))
    PD = _D
    B_wa = P.bufs_n(len(WGROUPS))
    B_wg = P.bufs_n(len(WGROUPS))
    B_wo = P.bufs_n(len(WGROUPS))
    wi_src = k.ffn_wi[widx].rearrange("(kc p) n -> p kc n", p=128)
    wo_src = k.ffn_wo[widx].rearrange("(j p) n -> p j n", p=128)
    for gi, (j0, nj) in enumerate(WGROUPS):
        PD.dma("pool", wi[:, :, j0 * 128:(j0 + nj) * 128], wi_src[:, :, j0 * 128:(j0 + nj) * 128],
               writes=[B_wa[gi]], key=f"{kp}wia{gi}")
        PD.dma("pool", wi[:, :, DFF + j0 * 128:DFF + (j0 + nj) * 128],
               wi_src[:, :, DFF + j0 * 128:DFF + (j0 + nj) * 128], writes=[B_wg[gi]], key=f"{kp}wig{gi}")
    for gi, (j0, nj) in enumerate(WGROUPS):
        PD.dma("pool", wo[:, j0:j0 + nj, :], wo_src[:, j0:j0 + nj, :], writes=[B_wo[gi]], key=f"{kp}wo{gi}")
    return B_wa, B_wg, B_wo


def widx_of(s):
    return (s // 3) * 2 + (0 if s % 3 == 0 else 1)


def phase_ffn(k, s, proj=None, WT=256, host_mod=None):
    P, nc = k.P, k.nc
    l, j = s // 3, s % 3
    first = (s == 0)
    last = (s == 5)
    KJ = NJ if proj is None else 8
    nb = 3 if proj is not None else (2 if WT == 256 else 1)
    with ExitStack() as st:
        def T(name, shape, dt):
            return st.enter_context(nc.sbuf_tensor(name, list(shape), dt))
        pf = f"f{s}_"
        if proj is None:
            wi, wo = alloc_ffn_w(k, st, pf)
            B_wa, B_wg, B_wo = ffn_weight_loads(k, widx_of(s), wi, wo)
            jgrp = JGRP
            hin = [T(pf + f"hin{i}", [128, 8, WT], BF16) for i in range(nb)]
            uT = T(pf + "uT", [128, NJ, WT], BF16)
            sg = [T(pf + f"sg{i}", [128, WT], F32) for i in range(2)]
        else:
            wo = T(pf + "wo", [128, 8, D], BF16)
            uin = [T(pf + f"uin{i}", [128, 8, WT], BF16) for i in range(nb)]
            B_wo = P.bufs_n(2)
            wsrc = (k.attn_wo if proj == "attn" else k.four_wo).rearrange("(j p) n -> p j n", p=128)
            for gi in range(2):
                P.dma("pool", wo[:, gi * 4:(gi + 1) * 4, :], wsrc[:, gi * 4:(gi + 1) * 4, :], writes=[B_wo[gi]],
                      key=f"wo{gi}")
            jgrp = [0, 0, 0, 0, 1, 1, 1, 1]
        xa = [T(pf + f"xa{i}", [128, 8, WT], F32) for i in range(nb)]
        ho = T(pf + "ho", [128, 8, WT], BF16)
        if first and nb == 1:
            stage = ho[:].rearrange("p c w -> p (c w)").bitcast(F32).rearrange("p (c w) -> p c w", c=8)
        zr = T(pf + "zr", [128, WT], F32R)
        zq = T(pf + "zq", [128, WT], F32R)
        mu = T(pf + "mu", [128, WT], F32)
        msq = T(pf + "msq", [128, WT], F32)
        nzt = 1 if (proj is None and WT == 512) else 2
        zt = [T(pf + f"zt{i}", [128, WT], F32) for i in range(nzt)]
        if proj is not None:
            zqs = [T(pf + f"zqs{i}", [128, WT], F32R) for i in range(3)]
        B_mu, B_msq = P.buf(), P.buf()
        if proj is None:
            zacc, zqacc, B_zacc, B_zqacc = mu, msq, B_mu, B_msq
        else:
            zacc, zqacc = T(pf + "zacc", [128, WT], F32), T(pf + "zqacc", [128, WT], F32)
            B_zacc, B_zqacc = P.buf(), P.buf()

        B_hin = P.bufs_n(nb)
        B_xa = [P.bufs_n(8) for _ in range(nb)]
        B_u = P.bufs_n(KJ)
        B_uin = P.bufs_n(nb)
        B_sg = P.bufs_n(2)
        B_ps = P.bufs_n(8)
        B_zr, B_zq = P.buf(), P.buf()
        B_zt = P.bufs_n(nzt)
        B_zqs = P.bufs_n(3)
        B_ho = P.bufs_n(8)
        extra = []
        if host_mod is not None:
            mwb = [T(pf + f"mw{i}", [128, 8, 1024], BF16) for i in range(2)]
            B_modh = P.buf()
            grps = mod_groups(k, host_mod, mwb, P.bufs_n(2), [B_ps[0], B_ps[1]], B_modh)
            extra = [[grps[0][0], grps[1][0]]]
            for t_ in range(5):
                batch = []
                for gi in (2 * t_, 2 * t_ + 1):
                    if gi < 9:
                        batch.append(grps[gi][1])
                        if gi + 2 < 9:
                            batch.append(grps[gi + 2][0])
                extra.append(batch)
            extra.append([lambda: mod_derive(k, host_mod, B_modh)])

        tiles = [("lat", t, (t * WT) // S) for t in range(TOK // WT)]
        if first:
            tiles += [("ctx", t, 2) for t in range(CTXN // WT)]
        NT = len(tiles)

        def dview(ap, t):
            return blkview(ap, t * WT, WT)

        def load_xa(ti):
            if ti >= NT:
                return
            kind, t, n = tiles[ti]
            sl = ti % nb
            if first:
                P.dma("sp", xa[sl][:], dview(k.xT if kind == "lat" else k.ctxT, t), writes=B_xa[sl], key=f"xa{sl}")
                for c in range(8 if nb == 2 else 0):
                    a_ap = k.DS[:, ds_col(0, 0, c, n): ds_col(0, 0, c, n) + 1]
                    b_ap = k.DS[:, ds_col(0, 1, c, n): ds_col(0, 1, c, n) + 1]
                    TS(P, "pool", hin[sl][:, c, :], xa[sl][:, c, :], a_ap, b_ap, ALU.mult, ALU.add,
                       [B_xa[sl][c]], [B_hin[sl]])
                for c in range(8):
                    TS(P, "pool", xa[sl][:, c, :], xa[sl][:, c, :], ALPHA, 0.0, ALU.mult, ALU.add, [], [B_xa[sl][c]])
            else:
                P.dma("sp", xa[sl][:], dview(k.xa_s, t), writes=B_xa[sl], key=f"xa{sl}")

        def load_in(ti):
            if ti >= NT or (first and nb == 2):
                return
            kind, t, n = tiles[ti]
            if first:
                for hf in range(2):
                    src = blkview(k.xT if kind == "lat" else k.ctxT, t * WT + hf * 256, 256)
                    P.dma("sp", stage, src, writes=B_ho, key="stage")
                    for c in range(8):
                        a_ap = k.DS[:, ds_col(0, 0, c, n): ds_col(0, 0, c, n) + 1]
                        b_ap = k.DS[:, ds_col(0, 1, c, n): ds_col(0, 1, c, n) + 1]
                        TS(P, "pool", hin[0][:, c, hf * 256:(hf + 1) * 256], stage[:, c, :], a_ap, b_ap, ALU.mult, ALU.add,
                           B_ho, [B_hin[0]])
                return
            if proj is None:
                P.dma("sp", hin[ti % nb][:], dview(k.hT_s, t), writes=[B_hin[ti % nb]], key=f"hin{ti % nb}")
            else:
                P.dma("sp", uin[ti % nb][:], dview(k.oT_s, t), writes=[B_uin[ti % nb]], key=f"uin{ti % nb}")

        qe = "dve" if first else "pool"
        pending = []
        epi = []

        def run_pending():
            while pending:
                pending.pop(0)()

        def run_epi(n=1):
            for _ in range(n):
                if epi:
                    epi.pop(0)()

        for t0 in range(nb):
            load_in(t0) if not (proj is None and nb == 1 and t0 > 0) else None
            load_xa(t0)
        for ti in range(NT):
            kind, t, n = tiles[ti]
            sl = ti % nb
            if proj is None:
                for jj in range(NJ):
                    ia, ig = (0, 1, 4)[jj % 3], (2, 3, 5)[jj % 3]
                    pa, pg = k.ps[ia], k.ps[ig]
                    Ba, Bg = B_ps[ia], B_ps[ig]
                    for kc in range(8):
                        MM(P, pa[:, 0:WT], wi[:, kc, jj * 128:(jj + 1) * 128], hin[sl][:, kc, :], kc == 0, kc == 7,
                           [B_wa[jgrp[jj]], B_hin[sl]], [Ba])
                    for kc in range(8):
                        MM(P, pg[:, 0:WT], wi[:, kc, DFF + jj * 128:DFF + (jj + 1) * 128], hin[sl][:, kc, :], kc == 0, kc == 7,
                           [B_wg[jgrp[jj]], B_hin[sl]], [Bg])
                    if jj == 1:
                        run_pending()
                    ACT(P, sg[jj % 2][:], pg[:, 0:WT], AF.Silu, [Bg], [B_sg[jj % 2]])
                    TT(P, "dve", uT[:, jj, :], pa[:, 0:WT], sg[jj % 2][:], ALU.mult, [Ba, B_sg[jj % 2]], [B_u[jj]])
                    if jj >= 1:
                        run_epi(1)
                run_epi(99)
                if nb == 1:
                    load_in(ti + 1)
                rhs_of = lambda jj: uT[:, jj, :]
                rhs_buf = lambda jj: B_u[jj]
            else:
                rhs_of = lambda jj, ti=ti: uin[ti % nb][:, jj, :]
                rhs_buf = lambda jj, ti=ti: B_uin[ti % nb]
            left0 = max(0, len(epi) - 11) if ti >= 1 else 0
            for c in range(8):
                py, By = k.ps[4 + c % 2], B_ps[4 + c % 2]
                for jj in range(KJ):
                    MM(P, py[:, 0:WT], wo[:, jj, c * 128:(c + 1) * 128], rhs_of(jj), jj == 0, jj == KJ - 1,
                       [B_wo[jgrp[jj]], rhs_buf(jj)], [By])
                if proj is not None:
                    while len(pending) > 1:
                        pending.pop(0)()
                g_ap = k.DS[:, ds_col(s, 2, c, n): ds_col(s, 2, c, n) + 1]
                zc = xa[sl][:, c, :]
                STT(P, "dve", zc, py[:, 0:WT], g_ap, zc, ALU.mult, ALU.add, [By, B_xa[sl][c]], [B_xa[sl][c]])
                if c == 0:
                    P.op("dve", (lambda zc=zc: lambda e: e.tensor_copy(zacc[:], zc))(), reads=[B_xa[sl][c]], writes=[B_zacc])
                else:
                    TT(P, "dve", zacc[:], zacc[:], zc, ALU.add, [B_zacc, B_xa[sl][c]], [B_zacc])
                if proj is not None:
                    zs, Bzs = zqs[c % 3], B_zqs[c % 3]
                    ACT(P, zs[:], zc, AF.Square, [B_xa[sl][c]], [Bzs])
                    psq, Bpsq = k.ps[2 + ti % 2], B_ps[2 + ti % 2]
                    pending.append((lambda c=c, zs=zs, Bzs=Bzs, psq=psq, Bpsq=Bpsq:
                                    lambda: MM(P, psq[:, 0:WT], k.ones_r[:], zs[:], c == 0, c == 7, [Bzs], [Bpsq]))())
                else:
                    ztc, Bztc = zt[c % nzt], B_zt[c % nzt]
                    ACT(P, ztc[:], zc, AF.Square, [B_xa[sl][c]], [Bztc])
                    if c == 0:
                        P.op(qe, (lambda ztc=ztc: lambda e: e.tensor_copy(zqacc[:], ztc[:]))(), reads=[Bztc], writes=[B_zqacc])
                    else:
                        TT(P, qe, zqacc[:], zqacc[:], ztc[:], ALU.add, [B_zqacc, Bztc], [B_zqacc])
                if proj is not None:
                    run_epi((left0, 0, 0, 1, 2, 2, 2, 2)[c])
            if extra:
                for f_ in extra.pop(0):
                    f_()
            ACT(P, zr[:], zacc[:], AF.Copy, [B_zacc], [B_zr])
            if proj is None:
                ACT(P, zq[:], zqacc[:], AF.Copy, [B_zqacc], [B_zq])
                pvar, Bpvar = k.ps[7], B_ps[7]
            else:
                pvar, Bpvar = k.ps[2 + ti % 2], B_ps[2 + ti % 2]

            def stats():
                MM(P, k.ps[6][:, 0:WT], k.ones_r[:], zr[:], True, True, [B_zr], [B_ps[6]])
                if proj is None:
                    MM(P, k.ps[7][:, 0:WT], k.ones_r[:], zq[:], True, True, [B_zq], [B_ps[7]])
            pending.append(stats)

            want_xa = not (kind == "ctx")

            def head(pvar=pvar, Bpvar=Bpvar):
                P.op("dve", lambda e: e.tensor_copy(mu[:], k.ps[6][:, 0:WT]), reads=[B_ps[6]], writes=[B_mu])
                TT(P, "dve", msq[:], mu[:], mu[:], ALU.mult, [B_mu], [B_msq])
                TT(P, "dve", msq[:], pvar[:, 0:WT], msq[:], ALU.subtract, [Bpvar, B_msq], [B_msq])
                ACT(P, msq[:], msq[:], AF.Sqrt, [B_msq], [B_msq], bias=k.epsT[:, 0:1], scale=1.0)

            def head2():
                P.op("dve", lambda e: e.reciprocal(msq[:], msq[:]), reads=[B_msq], writes=[B_msq])

            def chunk(c, sl=sl, n=n, want_xa=want_xa):
                zc = xa[sl][:, c, :]
                TT(P, "dve", zc, zc, mu[:], ALU.subtract, [B_xa[sl][c], B_mu], [B_xa[sl][c]])
                TT(P, "dve" if proj is None else "pool", zc, zc, msq[:], ALU.mult, [B_xa[sl][c], B_msq], [B_xa[sl][c]])
                lc = (l * 3 + j) * 8 + c
                if not last:
                    a_ap = k.DS[:, ds_col(s + 1, 0, c, n): ds_col(s + 1, 0, c, n) + 1]
                    b_ap = k.DS[:, ds_col(s + 1, 1, c, n): ds_col(s + 1, 1, c, n) + 1]
                    ACT(P, ho[:, c, :], zc, AF.Identity, [B_xa[sl][c]], [B_ho[c]], bias=b_ap, scale=a_ap)
                if want_xa:
                    if last:
                        ga, ba = k.LNC[:, lc:lc + 1], k.LNC[:, 48 + lc:48 + lc + 1]
                    else:
                        ga, ba = k.LNC[:, 96 + lc:96 + lc + 1], k.LNC[:, 144 + lc:144 + lc + 1]
                    if proj is None:
                        TS(P, "pool", zc, zc, ga, ba, ALU.mult, ALU.add, [B_xa[sl][c]], [B_xa[sl][c]])
                    else:
                        ACT(P, zc, zc, AF.Identity, [B_xa[sl][c]], [B_xa[sl][c]], bias=ba, scale=ga)

            def tail(ti=ti, sl=sl, kind=kind, t=t, want_xa=want_xa):
                if want_xa:
                    P.dma("sp", dview(k.yT if last else k.xa_s, t), xa[sl][:], reads=B_xa[sl], key=f"st_xo{sl}")
                if not last:
                    P.dma("sp", dview(k.hc_s if kind == "ctx" else k.hT_s, t), ho[:], reads=B_ho, key="st_ho")
                load_xa(ti + nb)
                if nb >= 2:
                    load_in(ti + nb)

            epi.append(head)
            epi.append(head2)
            for c in range(8):
                epi.append((lambda c=c, f=chunk: lambda: f(c))())
            epi.append(tail)
            if ti == NT - 1:
                run_pending()
                run_epi(99)
                while extra:
                    for f_ in extra.pop(0):
                        f_()
        P.barrier()
        P.flush()


def phase_attn(k):
    P, nc = k.P, k.nc
    with ExitStack() as st:
        def T(name, shape, dt):
            return st.enter_context(nc.sbuf_tensor(name, list(shape), dt))
        wq = T("a_wq", [128, 8, 1792], BF16)
        bqk = T("a_bqk", [128, 24], F32)
        bv = T("a_bv", [128, 256], F32)
        cosT = T("a_cos", [128, S], F32)
        sinT = T("a_sin", [128, S], F32)
        masks = T("a_masks", [128, 256], BF16)
        swm = T("a_swm", [128, 128], BF16)
        qpre = [T(f"a_qpre{i}", [128, 512], BF16) for i in range(2)]
        B_qpre = P.bufs_n(2)
        hb = [T(f"a_hb{i}", [128, 8, 512], BF16) for i in range(2)]
        hc = T("a_hc", [128, 8, 256], BF16)
        qT = T("a_qT", [128, 8, S], BF16)
        kTlo = T("a_kTlo", [128, 4, S], BF16)
        kThi = T("a_kThi", [128, 4, S], BF16)
        Vd = T("a_Vd", [128, 16, 512], BF16)
        kcTlo = T("a_kcTlo", [128, 4, 256], BF16)
        kcThi = T("a_kcThi", [128, 4, 256], BF16)
        Vc = T("a_Vc", [128, 2, 512], BF16)
        t1 = [T(f"a_t1{i}", [128, 512], F32) for i in range(2)]
        t2 = [T("a_t20", [128, 512], F32)] * 2
        PT = [T(f"a_PT{i}", [128, 512], BF16) for i in range(6)]
        R = [T(f"a_R{i}", [128, 512], F32) for i in range(2)]


        OT = [T("a_OT0", [128, 2, S], BF16)]

        B_w = [[] for _ in range(5)]
        B_c = P.buf()
        B_hb = P.bufs_n(2)
        B_hc = P.buf()
        B_q = [P.bufs_n(4) for _ in range(8)]
        B_k = [P.bufs_n(4) for _ in range(4)]
        B_V = P.bufs_n(16)
        B_kc, B_Vc = P.buf(), P.buf()
        B_t1 = P.bufs_n(2)
        B_t2 = [P.buf()] * 2
        B_PT = P.bufs_n(6)
        B_R = P.bufs_n(2)
        B_OT = P.bufs_n(1)
        B_kz = P.buf()
        B_ps = P.bufs_n(8)

        wsrc = k.wqkv.rearrange("(kc p) n -> p kc n", p=128)
        segs = [(0, 1024, 0), None, (2048, 2560, 1024), None, (3072, 3328, 1536)]
        for i, sg_ in enumerate(segs):
            if sg_ is None:
                continue
            a_, b_, o_ = sg_
            for h0 in range(a_, b_, 512):
                h1 = min(h0 + 512, b_)
                bw = P.buf()
                B_w[i].append(bw)
                P.dma("pool", wq[:, :, o_ + h0 - a_:o_ + h1 - a_], wsrc[:, :, h0:h1], writes=[bw], key=f"a_w{i}_{h0}")
        P.dma("sp", bqk[:], k.bqk, writes=[B_c], key="a_c0")
        P.dma("sp", bv[:], k.bv_bc, writes=[B_c], key="a_c1")
        P.dma("sp", cosT[:], k.cosT, writes=[B_c], key="a_c2")
        P.dma("sp", sinT[:], k.sinT, writes=[B_c], key="a_c3")
        P.dma("sp", masks[:], k.masks, writes=[B_c], key="a_c4")
        P.dma("sp", swm[:], k.SW, writes=[B_c], key="a_c5")
        P.op("pool", lambda e: e.memset(kTlo[:], 0.0), writes=[B_kz])
        P.op("pool", lambda e: e.memset(kThi[:], 0.0), writes=[B_kz])
        P.op("pool", lambda e: e.memset(kcTlo[:], 0.0), writes=[B_kz])
        P.op("pool", lambda e: e.memset(kcThi[:], 0.0), writes=[B_kz])
        QP, QS, KP, KS, VV = 0, 0, 1024, 1024, 1536
        it_rope = [0]

        rope_pending = []

        def flush_rope():
            while rope_pending:
                rope_pending.pop(0)()

        def proj_rope(dst, colp, cols, bp, bs, hbt, tt, Bw, Bdst):
            i = it_rope[0] % 2
            it_rope[0] += 1
            pA, pB = k.ps[2 * i], k.ps[2 * i + 1]
            for kc in range(8):
                MM(P, pA[:], wq[:, kc, colp:colp + 128], hbt[:, kc, :], kc == 0, kc == 7, Bw + [B_hb[tt % 2]], [B_ps[2 * i]])
            flush_rope()
            if _dbg("KT") != "2":
                ACT(P, qpre[i][:], pA[:], AF.Identity, [B_ps[2 * i], B_c], [B_qpre[i]], bias=bp, scale=1.0)
            STT(P, "dve", t1[i][:], pA[:], bp, cosT[:, tt * 512:(tt + 1) * 512], ALU.add, ALU.mult,
                [B_ps[2 * i], B_c, B_qpre[i]], [B_t1[i]])

            def rest(i=i, pB=pB, dst=dst, tt=tt, Bdst=Bdst):
                if _dbg("KT") != "1":
                    MM(P, pB[:], swm[:], qpre[i][:], True, True, [B_c, B_qpre[i]], [B_ps[2 * i + 1]])
                TT(P, "dve", t2[i][:], pB[:], sinT[:, tt * 512:(tt + 1) * 512], ALU.mult, [B_ps[2 * i + 1], B_c], [B_t2[i]])
                if isinstance(dst, list):
                    for (d_ap, hs_) in dst:
                        TT(P, "pool", d_ap, t1[i][hs_, :], t2[i][hs_, :], ALU.add, [B_t1[i], B_t2[i], B_kz], [Bdst])
                else:
                    TT(P, "pool", dst, t1[i][:], t2[i][:], ALU.add, [B_t1[i], B_t2[i]], [Bdst])
            rope_pending.append(rest)

        for b in range(2):
            P.dma("sp", hc[:], blkview(k.hc_s, b * 256, 256), writes=[B_hc], key="a_hc")
            P.dma("sp", hb[0][:], blkview(k.hT_s, b * S, 512), writes=[B_hb[0]], key="a_hb0")
            for tt in range(4):
                if tt + 1 < 4:
                    P.dma("sp", hb[(tt + 1) % 2][:], blkview(k.hT_s, b * S + (tt + 1) * 512, 512),
                          writes=[B_hb[(tt + 1) % 2]], key=f"a_hb{(tt + 1) % 2}")
                hbt = hb[tt % 2]
                for c in range(8):
                    proj_rope(qT[:, c, tt * 512:(tt + 1) * 512], QP + c * 128, QS + c * 128,
                              bqk[:, c:c + 1], bqk[:, 8 + c:9 + c], hbt, tt, B_w[0] + B_w[1], B_q[c][tt])
                for g in range(4):
                    proj_rope([(kTlo[0:64, g, tt * 512:(tt + 1) * 512], slice(0, 64)),
                               (kThi[64:128, g, tt * 512:(tt + 1) * 512], slice(64, 128))], KP + g * 128, KS + g * 128,
                              bqk[:, 16 + g:17 + g], bqk[:, 20 + g:21 + g], hbt, tt, B_w[2] + B_w[3], B_k[g][tt])
                flush_rope()
                for tb in range(4):
                    blk = tt * 4 + tb
                    pv, Bpv = k.ps[4 + tb % 2], B_ps[4 + tb % 2]
                    for kc in range(8):
                        MM(P, pv[:, 0:256], hbt[:, kc, tb * 128:(tb + 1) * 128], wq[:, kc, VV:VV + 256], kc == 0, kc == 7,
                           B_w[4] + [B_hb[tt % 2]], [Bpv])
                    o4 = Vd[:, blk, :].rearrange("p (g r d) -> p g r d", g=4, r=2)
                    i0 = pv[:, 0:256].rearrange("p (g d) -> p g d", g=4).unsqueeze(2).broadcast_to([128, 4, 2, 64])
                    i1 = bv[:, :].rearrange("p (g d) -> p g d", g=4).unsqueeze(2).broadcast_to([128, 4, 2, 64])
                    TT(P, "dve", o4, i0, i1, ALU.add, [Bpv, B_c], [B_V[blk]])
            for g in range(4):
                pk, Bpk = k.ps[6 + g % 2], B_ps[6 + g % 2]
                for kc in range(8):
                    MM(P, pk[:, 0:256], wq[:, kc, KP + g * 128:KP + (g + 1) * 128], hc[:, kc, :], kc == 0, kc == 7,
                       B_w[2] + [B_hc], [Bpk])
                ACT(P, kcTlo[0:64, g, :], pk[0:64, 0:256], AF.Identity, [Bpk, B_c, B_kz], [B_kc], bias=bqk[0:64, 16 + g:17 + g], scale=1.0)
                ACT(P, kcThi[64:128, g, :], pk[64:128, 0:256], AF.Identity, [Bpk, B_c, B_kz], [B_kc], bias=bqk[64:128, 16 + g:17 + g], scale=1.0)
            for cb in range(2):
                pv, Bpv = k.ps[4 + cb % 2], B_ps[4 + cb % 2]
                for kc in range(8):
                    MM(P, pv[:, 0:256], hc[:, kc, cb * 128:(cb + 1) * 128], wq[:, kc, VV:VV + 256], kc == 0, kc == 7,
                       B_w[4] + [B_hc], [Bpv])
                o4 = Vc[:, cb, :].rearrange("p (g r d) -> p g r d", g=4, r=2)
                i0 = pv[:, 0:256].rearrange("p (g d) -> p g d", g=4).unsqueeze(2).broadcast_to([128, 4, 2, 64])
                i1 = bv[:, :].rearrange("p (g d) -> p g d", g=4).unsqueeze(2).broadcast_to([128, 4, 2, 64])
                TT(P, "dve", o4, i0, i1, ALU.add, [Bpv, B_c], [B_Vc])
            it = 0
            norm_pending = []
            dbg = _dbg("KDBG", "")
            for g in range(0 if dbg == "a1" else (1 if dbg.startswith("a2") else 4)):
                osl = 0
                for qb in range(2 if dbg.startswith("a2") else 16):
                    tq = qb // 4
                    srcs = []
                    items = [("c", 0, None), ("c", 1, None)]
                    for d in (-1, 0, 1):
                        kb = qb + d
                        if 0 <= kb < 16:
                            items.append(("w", kb, None if d == 0 else (0 if d == -1 else 1)))
                    for (kind, kb, mk) in items:
                        slot = it % 6
                        psn = it % 2
                        it += 1
                        pss, Bpss = k.ps[psn], B_ps[psn]
                        for half in range(2):
                            if kind == "c":
                                lhsT = (kcTlo if half == 0 else kcThi)[:, g, kb * 128:(kb + 1) * 128]
                                rd = [B_kc, B_q[2 * g][tq], B_q[2 * g + 1][tq]]
                            else:
                                lhsT = (kTlo if half == 0 else kThi)[:, g, kb * 128:(kb + 1) * 128]
                                rd = [B_k[g][kb // 4], B_q[2 * g][tq], B_q[2 * g + 1][tq]]
                            MM(P, pss[:, half * 256:(half + 1) * 256].rearrange("p (c q) -> p c q", q=128), lhsT,
                               qT[:, 2 * g:2 * g + 2, qb * 128:(qb + 1) * 128], True, True, rd, [Bpss])
                        ACT(P, PT[slot][:], pss[:], AF.Exp, [Bpss], [B_PT[slot]], bias=0.0, scale=0.125)
                        if mk is not None and dbg != "a2_1":
                            m3 = bc_mid(masks[:, mk * 128:(mk + 1) * 128], 4)
                            p3 = PT[slot][:].rearrange("p (i q) -> p i q", q=128)
                            TT(P, "pool", p3, p3, m3, ALU.mult, [B_PT[slot], B_c], [B_PT[slot]])
                        vl = Vc[:, kb, g * 128:(g + 1) * 128] if kind == "c" else Vd[:, kb, g * 128:(g + 1) * 128]
                        vb = B_Vc if kind == "c" else B_V[kb]
                        srcs.append((slot, vl, vb))
                    if dbg in ("a2_1", "a2_2"):
                        continue
                    while norm_pending:
                        norm_pending.pop(0)()
                    po, Bpo = k.ps[2 + qb % 2], B_ps[2 + qb % 2]
                    pm, Bpm = k.ps[4 + qb % 2], B_ps[4 + qb % 2]
                    for ii, (slot, vl, vb) in enumerate(srcs):
                        MM(P, po[:], vl, PT[slot][:], ii == 0, ii == len(srcs) - 1, [vb, B_PT[slot]], [Bpo])
                        MM(P, pm[:], k.ones_b[:], PT[slot][:], ii == 0, ii == len(srcs) - 1, [B_PT[slot]], [Bpm])

                    def norm(g=g, qb=qb, po=po, Bpo=Bpo, pm=pm, Bpm=Bpm, osl=osl):
                        r = R[qb % 2]
                        use_act = True
                        for cb in range(4):
                            i = 2 * (cb % 2) + cb // 2
                            es = k.esink[:, 4 * g + i:4 * g + i + 1]
                            if use_act:
                                ACT(P, r[:, cb * 128:(cb + 1) * 128], pm[:, cb * 128:(cb + 1) * 128], AF.Ln, [Bpm], [B_R[qb % 2]],
                                    bias=es, scale=1.0)
                            else:
                                P.op("dve", (lambda o=r[:, cb * 128:(cb + 1) * 128], a=pm[:, cb * 128:(cb + 1) * 128], es=es:
                                             lambda e: e.tensor_single_scalar(o, a, es, ALU.add))(), reads=[Bpm], writes=[B_R[qb % 2]])
                        if use_act:
                            ACT(P, r[:], r[:], AF.Exp, [B_R[qb % 2]], [B_R[qb % 2]], bias=0.0, scale=-1.0)
                        else:
                            P.op("dve", (lambda r=r: lambda e: e.reciprocal(r[:], r[:]))(), reads=[B_R[qb % 2]], writes=[B_R[qb % 2]])
                        for cb in range(4):
                            i = 2 * (cb % 2) + cb // 2
                            half = i % 2
                            hs = slice(half * 64, (half + 1) * 64)
                            TT(P, "dve", OT[osl][hs, i // 2, qb * 128:(qb + 1) * 128], po[hs, cb * 128:(cb + 1) * 128],
                               r[hs, cb * 128:(cb + 1) * 128], ALU.mult, [Bpo, B_R[qb % 2]], [B_OT[osl]])
                    norm_pending.append(norm)
                while norm_pending:
                    norm_pending.pop(0)()
                for tt in range(4):
                    dst = blkview(k.oT_s, b * S + tt * 512, 512)[:, 2 * g:2 * g + 2, :]
                    P.dma("sp", dst, OT[osl][:, :, tt * 512:(tt + 1) * 512], reads=[B_OT[osl]], key=f"a_ot{tt}")
        P.barrier()
        P.flush()


def phase_fourier(k):
    P, nc = k.P, k.nc
    with ExitStack() as st:
        def T(name, shape, dt):
            return st.enter_context(nc.sbuf_tensor(name, list(shape), dt))
        csc = T("u_csc", [128, 2, 512], BF16)
        fm = T("u_fm", [128, 5, 128], BF16)
        hb = [T(f"u_hb{i}", [128, 8, 512], BF16) for i in range(2)]
        A = T("u_A", [128, 16, D], BF16)
        Bm = T("u_B", [128, 16, D], BF16)
        Az = T("u_Az", [128, D], BF16)
        ctab = [T(f"u_ct{i}", [128, 9, 512], BF16) for i in range(2)]
        stab = [T(f"u_st{i}", [128, 8, 512], BF16) for i in range(2)]
        fo = [T(f"u_fo{i}", [128, 512], BF16) for i in range(2)]
        B_csc = P.buf()
        B_hb = P.bufs_n(2)
        B_A = P.bufs_n(16)
        B_Az = P.buf()
        B_tab = P.bufs_n(2)
        B_fo = P.bufs_n(2)
        B_ps = P.bufs_n(8)
        P.dma("sp", csc[:], k.CSC.rearrange("(kc p) n -> p kc n", p=128), writes=[B_csc], key="u_csc")
        P.dma("sp", fm[:], k.FM.rearrange("p (j m) -> p j m", j=5), writes=[B_csc], key="u_fm")
        clsrc = k.CL.rearrange("(lb p) n -> p lb n", p=128)
        slsrc = k.SLn.rearrange("(lb p) n -> p lb n", p=128)
        ti = 0
        fi = 0
        ei = 0
        for b in range(2):
            def load_tab(kt, sl):
                P.dma("sp", ctab[sl][:], clsrc[:, 0:9, kt * 512:(kt + 1) * 512], writes=[B_tab[sl]], key=f"u_ct{sl}")
                P.dma("sp", stab[sl][:], slsrc[:, 0:8, kt * 512:(kt + 1) * 512], writes=[B_tab[sl]], key=f"u_st{sl}")
            load_tab(0, ti % 2)
            P.dma("sp", hb[0][:], blkview(k.hT_s, b * S, 512), writes=[B_hb[0]], key="u_hb0")
            for tt in range(4):
                if tt + 1 < 4:
                    P.dma("sp", hb[(tt + 1) % 2][:], blkview(k.hT_s, b * S + (tt + 1) * 512, 512),
                          writes=[B_hb[(tt + 1) % 2]], key=f"u_hb{(tt + 1) % 2}")
                for tb in range(4):
                    blk = tt * 4 + tb
                    for gq in range(4):
                        pi = (blk * 4 + gq) % 2
                        pp, Bpp = k.ps[pi], B_ps[pi]
                        for kc in range(2):
                            MM(P, pp[:], hb[tt % 2][:, gq * 2 + kc, tb * 128:(tb + 1) * 128], csc[:, kc, :], kc == 0, kc == 1,
                               [B_hb[tt % 2], B_csc], [Bpp])
                        ACT(P, A[:, blk, gq * 256:(gq + 1) * 256], pp[:, 0:256], AF.Copy, [Bpp], [B_A[blk]])
                        P.op("dve", (lambda o=Bm[:, blk, gq * 256:(gq + 1) * 256], i=pp[:, 256:512]:
                                     lambda e: e.tensor_copy(o, i))(), reads=[Bpp], writes=[B_A[blk]])
            def evac(out_ap, pp, Bpp, Bout):
                nonlocal ei
                if ei % 2 == 0:
                    ACT(P, out_ap, pp[:], AF.Copy, [Bpp], [Bout])
                else:
                    P.op("dve", (lambda o=out_ap, i=pp[:]: lambda e: e.tensor_copy(o, i))(), reads=[Bpp], writes=[Bout])
                ei += 1
            for hf in range(2):
                pp, Bpp = k.ps[4 + hf], B_ps[4 + hf]
                MM(P, pp[:], fm[:, 2, :], A[:, 8, hf * 512:(hf + 1) * 512], True, True, [B_csc, B_A[8]], [Bpp])
                evac(Az[:, hf * 512:(hf + 1) * 512], pp, Bpp, B_Az)
            for lb in range(8):
                for hf in range(2):
                    cs = slice(hf * 512, (hf + 1) * 512)
                    for (X, jn, en) in ((A, 1, 2), (Bm, 3, 4)):
                        pi = 4 + ei % 2
                        pp, Bpp = k.ps[pi], B_ps[pi]
                        MM(P, pp[:], fm[:, 0, :], X[:, lb, cs], True, False, [B_csc, B_A[lb]], [Bpp])
                        MM(P, pp[:], fm[:, jn, :], X[:, 15 - lb, cs], False, lb == 0, [B_csc, B_A[15 - lb]], [Bpp])
                        if lb >= 1:
                            MM(P, pp[:], fm[:, en, :], X[:, 16 - lb, cs], False, True, [B_csc, B_A[16 - lb]], [Bpp])
                        evac(X[:, lb, cs], pp, Bpp, B_A[lb])
            for kt in range(4):
                sl = ti % 2
                ti += 1
                if kt + 1 < 4:
                    load_tab(kt + 1, ti % 2)
                for cc in range(8):
                    pi = 2 + fi % 2
                    pp, Bpp = k.ps[pi], B_ps[pi]
                    for lb in range(8):
                        MM(P, pp[:], A[:, lb, cc * 128:(cc + 1) * 128], ctab[sl][:, lb, :], lb == 0, False,
                           [B_A[lb], B_tab[sl]], [Bpp])
                        MM(P, pp[:], Bm[:, lb, cc * 128:(cc + 1) * 128], stab[sl][:, lb, :], False, False,
                           [B_A[lb], B_tab[sl]], [Bpp])
                    MM(P, pp[:], Az[:, cc * 128:(cc + 1) * 128], ctab[sl][:, 8, :], False, True, [B_Az, B_tab[sl]], [Bpp])
                    fs = fi % 2
                    fi += 1
                    if fs == 0:
                        ACT(P, fo[fs][:], pp[:], AF.Copy, [Bpp], [B_fo[fs]])
                    else:
                        P.op("dve", (lambda o=fo[fs][:], i=pp[:]: lambda e: e.tensor_copy(o, i))(), reads=[Bpp], writes=[B_fo[fs]])
                    P.dma("sp", blkview(k.oT_s, b * S + kt * 512, 512)[:, cc, :], fo[fs][:], reads=[B_fo[fs]], key=f"u_fo{fs}")
        P.barrier()
        P.flush()


def _const_tables():
    bf = ml_dtypes.bfloat16
    t = np.arange(S)
    row = (t // 64).astype(np.float64)
    col = (t % 64).astype(np.float64)
    inv = 10000.0 ** (-np.arange(0, 32, 2, dtype=np.float64) / 32.0)
    cosT = np.zeros((128, S), np.float64)
    sinT = np.zeros((128, S), np.float64)
    for p in range(128):
        r = p % 64
        part, axis, i = r // 32, (r % 32) // 16, r % 16
        ang = (row if axis == 0 else col) * inv[i]
        cosT[p] = np.cos(ang)
        sinT[p] = np.sin(ang) * (-1.0 if part == 0 else 1.0)
    kk = np.arange(128)
    m_prev = (kk[None, :] <= kk[:, None]).astype(np.float32)
    m_next = (kk[:, None] <= kk[None, :]).astype(np.float32)
    masks = np.concatenate([m_prev, m_next], axis=1).astype(bf)
    ll = np.arange(S, dtype=np.float64)
    angL = 2.0 * np.pi * ((ll[:, None] * ll[None, :]) % S) / S
    CL = (np.cos(angL) / np.sqrt(S)).astype(np.float32).astype(bf)
    SLn = (-np.sin(angL) / np.sqrt(S)).astype(np.float32).astype(bf)
    cc = np.arange(256, dtype=np.float64)
    angC = 2.0 * np.pi * ((cc[:, None] * cc[None, :]) % 256) / 256
    CSC = np.concatenate([np.cos(angC) / 16.0, np.sin(angC) / 16.0], axis=1).astype(np.float32).astype(bf)
    eye = np.eye(128, dtype=np.float32)
    J1 = np.zeros((128, 128), np.float32)
    for m in range(1, 128):
        J1[128 - m, m] = 1.0
    E0 = np.zeros((128, 128), np.float32)
    E0[0, 0] = 1.0
    FM = np.concatenate([eye, J1, E0, -J1, -E0], axis=1).astype(bf)
    SW = np.zeros((128, 128), np.float32)
    for m in range(128):
        SW[m ^ 32, m] = 1.0
    SW = SW.astype(bf)
    return cosT.astype(np.float32), sinT.astype(np.float32), masks, CL, SLn, CSC, FM, SW


def _head_perm():
    perm = np.zeros(64, np.int64)
    swp = np.zeros(64, np.int64)
    for r in range(64):
        part, axis, i = r // 32, (r % 32) // 16, r % 16
        perm[r] = axis * 32 + part * 16 + i
        swp[r] = axis * 32 + (1 - part) * 16 + i
    return perm, swp


def prepare_inputs(x, c, ctx, c_ctx, mod_w, mod_b, ln_g, ln_b, ffn_wi, ffn_wo,
                   attn_wqkv, attn_bqkv, attn_wo, attn_sink, fourier_wo):
    f32 = np.float32
    x = np.asarray(x, f32); c = np.asarray(c, f32); ctx = np.asarray(ctx, f32); c_ctx = np.asarray(c_ctx, f32)
    cosT, sinT, masks, CL, SLn, CSC, FM, SW = _const_tables()
    perm, swp = _head_perm()
    wqkv = np.asarray(attn_wqkv, f32)[0]
    bqkv = np.asarray(attn_bqkv, f32)[0]
    qcols_p = np.concatenate([h * 64 + perm for h in range(16)])
    qcols_s = np.concatenate([h * 64 + swp for h in range(16)])
    kcols_p = np.concatenate([np.concatenate([1024 + g * 64 + perm] * 2) for g in range(4)])
    kcols_s = np.concatenate([np.concatenate([1024 + g * 64 + swp] * 2) for g in range(4)])
    vcols = 1280 + np.arange(256)
    allc = np.concatenate([qcols_p, qcols_s, kcols_p, kcols_s, vcols])
    wq_x = np.ascontiguousarray(wqkv[:, allc])
    bq_p = bqkv[qcols_p].reshape(8, 128).T
    bq_s = bqkv[qcols_s].reshape(8, 128).T
    bk_p = bqkv[kcols_p].reshape(4, 128).T
    bk_s = bqkv[kcols_s].reshape(4, 128).T
    bqk = np.ascontiguousarray(np.concatenate([bq_p, bq_s, bk_p, bk_s], axis=1))
    bv_bc = np.ascontiguousarray(np.broadcast_to(bqkv[vcols][None, :], (128, 256)))
    sink_bc = np.ascontiguousarray(np.broadcast_to(np.asarray(attn_sink, f32)[0][None, :], (128, 16)))
    mod_bT = np.ascontiguousarray(np.concatenate([np.asarray(mod_b, f32)[l].reshape(72, 128).T for l in range(2)], axis=1))
    lngT = np.ascontiguousarray(np.asarray(ln_g, f32).reshape(48, 128).T)
    lnbT = np.ascontiguousarray(np.asarray(ln_b, f32).reshape(48, 128).T)
    shared = dict(
        mod_w=np.ascontiguousarray(np.asarray(mod_w, f32)), mod_bT=mod_bT, lngT=lngT, lnbT=lnbT,
        ffn_wi=np.ascontiguousarray(np.asarray(ffn_wi, f32).reshape(4, D, 2 * DFF)),
        ffn_wo=np.ascontiguousarray(np.asarray(ffn_wo, f32).reshape(4, DFF, D)),
        wqkv=wq_x, bqk=bqk, bv_bc=bv_bc, attn_wo=np.ascontiguousarray(np.asarray(attn_wo, f32)[0]), sink_bc=sink_bc,
        four_wo=np.ascontiguousarray(np.asarray(fourier_wo, f32)[0]),
        cosT=cosT, sinT=sinT, masks=masks, CL=CL, SLn=SLn, CSC=CSC, FM=FM, SW=SW,
    )
    in_maps = []
    for core in range(NCORES):
        b0 = 2 * core
        xT = np.ascontiguousarray(x[b0:b0 + 2].reshape(8, 512, 8, 128).transpose(0, 3, 2, 1)).reshape(8, 128, 4096)
        ctxT = np.ascontiguousarray(ctx[b0:b0 + 2].reshape(1, 512, 8, 128).transpose(0, 3, 2, 1)).reshape(1, 128, 4096)
        cv = np.stack([c[b0], c[b0 + 1], c_ctx], axis=1)
        cT = np.ascontiguousarray(cv.reshape(8, 128, 3).transpose(1, 0, 2).reshape(128, 24))
        m = dict(shared)
        m.update(xT=xT, ctxT=ctxT, cT=cT)
        in_maps.append(m)
    return in_maps


def unblock(yT):
    return np.asarray(yT).reshape(8, 128, 8, 512).transpose(0, 3, 2, 1).reshape(2, S, D)


_NC_CACHE = {}


def kernel(x, c, ctx, c_ctx, mod_w, mod_b, ln_g, ln_b, ffn_wi, ffn_wo,
           attn_wqkv, attn_bqkv, attn_wo, attn_sink, fourier_wo):
    in_maps = prepare_inputs(x, c, ctx, c_ctx, mod_w, mod_b, ln_g, ln_b, ffn_wi, ffn_wo,
                             attn_wqkv, attn_bqkv, attn_wo, attn_sink, fourier_wo)
    if "nc" not in _NC_CACHE:
        _NC_CACHE["nc"] = build()
    nc = _NC_CACHE["nc"]
    res = run_bass_kernel_spmd(nc, in_maps, core_ids=list(range(NCORES)))
    out = np.empty((16, S, D), np.float32)
    for core in range(NCORES):
        yT = res.results[core]["yT"]
        out[2 * core:2 * core + 2] = unblock(yT)
    return out
```
